# Optimizing a Trainium2 kernel written in Bass

```python
import math
import jax, jax.numpy as jnp
from jax import lax
import numpy as np

D_MODEL = 2048
BATCH = 32
SEQ = 256
DEPTH = 2
DEC_BATCH = 4
DEC_SEQ = 1024
PAST_LEN = 512

GRID_W = 64
N_EVEN = (DEPTH + 1) // 2
N_ODD = DEPTH // 2
QBLK = 128
ROPE_BASE = 10000.0
EPS = 1e-6
NEG_INF = -1e30

DA_HEADS = 8
DA_QK_DIM = 64
DA_V_DIM = 2 * DA_QK_DIM
MLA_HEADS = 8
MLA_Q_RANK = 512
MLA_KV_RANK = 512
MLA_NOPE = 128
MLA_ROPE = 64
MLA_V = 128
MLA_QK = MLA_NOPE + MLA_ROPE
GQ_HEADS = 16
GQ_KV_HEADS = 4
GQ_GROUP = GQ_HEADS // GQ_KV_HEADS
GQ_DIM = 128
WINDOW = 128
D_FF = 4 * D_MODEL

DA_QW = DA_HEADS * 2 * DA_QK_DIM
DA_VW = DA_HEADS * DA_V_DIM
AB_SPLITS = [DA_QW, 2 * DA_QW, 2 * DA_QW + DA_VW,
             2 * DA_QW + DA_VW + MLA_Q_RANK,
             2 * DA_QW + DA_VW + MLA_Q_RANK + MLA_KV_RANK]
AB_IN = 2 * DA_QW + DA_VW + MLA_Q_RANK + MLA_KV_RANK + MLA_ROPE
AB_OUT = DA_VW + MLA_HEADS * MLA_V
C_SPLITS = [GQ_HEADS * GQ_DIM, GQ_HEADS * GQ_DIM + GQ_KV_HEADS * GQ_DIM]
C_IN = GQ_HEADS * GQ_DIM + 2 * GQ_KV_HEADS * GQ_DIM
C_OUT = GQ_HEADS * GQ_DIM

kernel_name = "hybrid_diffusion_prefix_ctx_step"


def rmsnorm(x, g):
    xf = x.astype(jnp.float32)
    y = xf * lax.rsqrt(jnp.mean(xf * xf, axis=-1, keepdims=True) + EPS)
    return (y * g.astype(jnp.float32)).astype(x.dtype)


def rope_1d(x, pos):
    half = x.shape[-1] // 2
    inv = ROPE_BASE ** (-jnp.arange(half, dtype=jnp.float32) / half)
    ang = pos.astype(jnp.float32)[:, None] * inv
    cos = jnp.cos(ang)[:, None, :].astype(x.dtype)
    sin = jnp.sin(ang)[:, None, :].astype(x.dtype)
    x1, x2 = x[..., :half], x[..., half:]
    return jnp.concatenate([x1 * cos - x2 * sin, x2 * cos + x1 * sin], axis=-1)


def axial_rope(x, pos):
    a = x.shape[-1] // 2
    return jnp.concatenate([rope_1d(x[..., :a], pos[0]), rope_1d(x[..., a:], pos[1])], axis=-1)


def to_blocks(x):
    b, s = x.shape[:2]
    return jnp.swapaxes(x.reshape(b, s // QBLK, QBLK, *x.shape[2:]), 0, 1)


def from_blocks(y):
    nb, b, q = y.shape[:3]
    return jnp.swapaxes(y, 0, 1).reshape(b, nb * q, *y.shape[3:])


def diff_attention(q1, q2, k1, k2, v, lam):
    scale = DA_QK_DIM ** -0.5

    def block(qs):
        b1, b2 = qs
        p1 = jax.nn.softmax(jnp.einsum('bqhd,bkhd->bhqk', b1, k1).astype(jnp.float32) * scale, axis=-1)
        p2 = jax.nn.softmax(jnp.einsum('bqhd,bkhd->bhqk', b2, k2).astype(jnp.float32) * scale, axis=-1)
        a = (p1 - lam * p2).astype(v.dtype)
        return jnp.einsum('bhqk,bkhd->bqhd', a, v)

    return from_blocks(lax.map(block, (to_blocks(q1), to_blocks(q2))))


def softmax_attention(q, k, v):
    scale = q.shape[-1] ** -0.5

    def block(qb):
        p = jax.nn.softmax(jnp.einsum('bqhd,bkhd->bhqk', qb, k).astype(jnp.float32) * scale, axis=-1)
        return jnp.einsum('bhqk,bkhd->bqhd', p.astype(v.dtype), v)

    return from_blocks(lax.map(block, to_blocks(q)))


def sink_window_attention(q, k_ctx, v_ctx, sink, k_lat=None, v_lat=None):
    b, s = q.shape[:2]
    lc = k_ctx.shape[1]
    scale = GQ_DIM ** -0.5
    band = QBLK + 2 * WINDOW
    qg = q.reshape(b, s, GQ_KV_HEADS, GQ_GROUP, GQ_DIM)
    sink_col = sink.astype(jnp.float32).reshape(1, GQ_KV_HEADS, GQ_GROUP, 1, 1)
    if k_lat is not None:
        pad = ((0, 0), (WINDOW, WINDOW), (0, 0), (0, 0))
        kp = jnp.pad(k_lat, pad)
        vp = jnp.pad(v_lat, pad)

    def block(args):
        qb, bi = args
        s_ctx = jnp.einsum('bqhgd,bkhd->bhgqk', qb, k_ctx).astype(jnp.float32) * scale
        sinks = jnp.broadcast_to(sink_col, s_ctx.shape[:-1] + (1,))
        if k_lat is None:
            p = jax.nn.softmax(jnp.concatenate([s_ctx, sinks], axis=-1), axis=-1)
            return jnp.einsum('bhgqk,bkhd->bqhgd', p[..., :lc].astype(v_ctx.dtype), v_ctx)
        kb = lax.dynamic_slice_in_dim(kp, bi * QBLK, band, axis=1)
        vb = lax.dynamic_slice_in_dim(vp, bi * QBLK, band, axis=1)
        s_loc = jnp.einsum('bqhgd,bkhd->bhgqk', qb, kb).astype(jnp.float32) * scale
        qi = bi * QBLK + jnp.arange(QBLK)
        kj = bi * QBLK - WINDOW + jnp.arange(band)
        valid = (jnp.abs(qi[:, None] - kj[None, :]) <= WINDOW) & (kj >= 0)[None, :] & (kj < s)[None, :]
        s_loc = jnp.where(valid, s_loc, NEG_INF)
        p = jax.nn.softmax(jnp.concatenate([s_ctx, s_loc, sinks], axis=-1), axis=-1)
        o = jnp.einsum('bhgqk,bkhd->bqhgd', p[..., :lc].astype(v_ctx.dtype), v_ctx)
        return o + jnp.einsum('bhgqk,bkhd->bqhgd', p[..., lc:lc + band].astype(vb.dtype), vb)

    out = lax.map(block, (to_blocks(qg), jnp.arange(s // QBLK)))
    return from_blocks(out).reshape(b, s, GQ_HEADS * GQ_DIM)


def mla_keys_values(ckv, kr, w_kv_up, k_norm):
    b, l = ckv.shape[:2]
    kv = (ckv @ w_kv_up).reshape(b, l, MLA_HEADS, MLA_NOPE + MLA_V)
    k_nope, v = kv[..., :MLA_NOPE], kv[..., MLA_NOPE:]
    k_rope = jnp.broadcast_to(kr[:, :, None, :], (b, l, MLA_HEADS, MLA_ROPE))
    k = rmsnorm(jnp.concatenate([k_nope, k_rope], axis=-1), k_norm)
    return k, v


def rope_tail(t, pos):
    return jnp.concatenate([t[..., :MLA_NOPE], axial_rope(t[..., MLA_NOPE:], pos)], axis=-1)


def mixer_ab(h, pos, ctx, w_in, w_out, lq1, lk1, lq2, lk2, da_qn, da_kn, da_subln, lam_init,
             mq_norm, w_q_up, mkv_norm, w_kv_up, mla_qn, mla_kn):
    b, s, _ = h.shape
    dq, dk, dv, mq, mkv, mkr = jnp.split(h @ w_in, AB_SPLITS, axis=-1)
    dq = rmsnorm(dq.reshape(b, s, DA_HEADS, 2, DA_QK_DIM), da_qn)
    dk = rmsnorm(dk.reshape(b, s, DA_HEADS, 2, DA_QK_DIM), da_kn)
    dv = dv.reshape(b, s, DA_HEADS, DA_V_DIM)
    q1, q2, k1, k2 = dq[:, :, :, 0], dq[:, :, :, 1], dk[:, :, :, 0], dk[:, :, :, 1]
    q = rmsnorm((rmsnorm(mq, mq_norm) @ w_q_up).reshape(b, s, MLA_HEADS, MLA_QK), mla_qn)
    ckv = rmsnorm(mkv, mkv_norm)
    k, v = mla_keys_values(ckv, mkr, w_kv_up, mla_kn)
    if ctx is None:
        new = (jnp.concatenate([k1, k2], axis=-1), dv, ckv, mkr)
    else:
        c_dak, c_dav, c_ckv, c_kr = ctx
        q1, q2, k1, k2 = [axial_rope(t, pos) for t in (q1, q2, k1, k2)]
        k1 = jnp.concatenate([c_dak[..., :DA_QK_DIM], k1], axis=1)
        k2 = jnp.concatenate([c_dak[..., DA_QK_DIM:], k2], axis=1)
        dv = jnp.concatenate([c_dav, dv], axis=1)
        q = rope_tail(q, pos)
        ck, cv = mla_keys_values(c_ckv, c_kr, w_kv_up, mla_kn)
        k = jnp.concatenate([ck, rope_tail(k, pos)], axis=1)
        v = jnp.concatenate([cv, v], axis=1)
        new = None
    f32 = jnp.float32
    lam = (jnp.exp(jnp.sum(lq1.astype(f32) * lk1.astype(f32)))
           - jnp.exp(jnp.sum(lq2.astype(f32) * lk2.astype(f32))) + lam_init)
    o_da = rmsnorm(diff_attention(q1, q2, k1, k2, dv, lam), da_subln) * (1.0 - lam_init)
    o_mla = softmax_attention(q, k, v)
    o = jnp.concatenate([o_da.reshape(b, s, DA_VW), o_mla.reshape(b, s, MLA_HEADS * MLA_V)], axis=-1)
    return o @ w_out, new


def mixer_c(h, pos, ctx, w_in, w_out, qn, kn, sink):
    b, s, _ = h.shape
    q, k, v = jnp.split(h @ w_in, C_SPLITS, axis=-1)
    q = rmsnorm(q.reshape(b, s, GQ_HEADS, GQ_DIM), qn)
    k = rmsnorm(k.reshape(b, s, GQ_KV_HEADS, GQ_DIM), kn)
    v = v.reshape(b, s, GQ_KV_HEADS, GQ_DIM)
    if ctx is None:
        o = sink_window_attention(q, k, v, sink)
        new = (k, v)
    else:
        o = sink_window_attention(axial_rope(q, pos), ctx[0], ctx[1], sink,
                                  axial_rope(k, pos), v)
        new = None
    return o @ w_out, new


def sqrelu_mlp(h, w1, w2):
    return jnp.square(jax.nn.relu(h @ w1)) @ w2


def run_trunk(x, cond, pos, cache, P):
    new = {"da_k": [], "da_v": [], "mla_ckv": [], "mla_krope": [], "gq_k": [], "gq_v": []}
    for l in range(DEPTH):
        m = (jax.nn.silu(cond) @ P["ada_w"][l] + P["ada_b"][l]).reshape(-1, 1, 6 * D_MODEL)
        sh1, sc1, g1, sh2, sc2, g2 = jnp.split(m, 6, axis=-1)
        h = rmsnorm(x, P["norm1_g"][l]) * (1 + sc1) + sh1
        i = l // 2
        if l % 2 == 0:
            ctx = None if cache is None else (cache["da_k"][:, i], cache["da_v"][:, i],
                                              cache["mla_ckv"][:, i], cache["mla_krope"][:, i])
            y, nc = mixer_ab(h, pos, ctx, P["ab_w_in"][i], P["ab_w_out"][i],
                             P["da_lambda_q1"][i], P["da_lambda_k1"][i],
                             P["da_lambda_q2"][i], P["da_lambda_k2"][i],
                             P["da_q_norm"][i], P["da_k_norm"][i], P["da_subln"][i],
                             0.8 - 0.6 * math.exp(-0.3 * l),
                             P["mla_q_a_norm"][i], P["mla_w_q_up"][i],
                             P["mla_kv_a_norm"][i], P["mla_w_kv_up"][i],
                             P["mla_q_norm"][i], P["mla_k_norm"][i])
            if nc is not None:
                for name, t in zip(("da_k", "da_v", "mla_ckv", "mla_krope"), nc):
                    new[name].append(t)
        else:
            ctx = None if cache is None else (cache["gq_k"][:, i], cache["gq_v"][:, i])
            y, nc = mixer_c(h, pos, ctx, P["c_w_in"][i], P["c_w_out"][i],
                            P["gq_q_norm"][i], P["gq_k_norm"][i], P["gq_sink"][i])
            if nc is not None:
                new["gq_k"].append(nc[0])
                new["gq_v"].append(nc[1])
        x = x + g1 * y
        h = rmsnorm(x, P["norm2_g"][l]) * (1 + sc2) + sh2
        x = x + g2 * sqrelu_mlp(h, P["ff1_w"][l], P["ff2_w"][l])
    return x, new


def setup_inputs(seed: int = 0) -> dict:
    key = jax.random.key(seed)
    ks = iter(jax.random.split(key, 40))
    f32 = jnp.float32

    def nrm(shape, scale=1.0):
        return jax.random.normal(next(ks), shape, f32) * scale

    def gain(shape):
        return 1.0 + 0.02 * jax.random.normal(next(ks), shape, f32)

    D = D_MODEL
    return {
        "x_prompt": nrm((BATCH, SEQ, D)),
        "x_sample": nrm((DEC_BATCH, DEC_SEQ, D)),
        "cache_da_k": nrm((DEC_BATCH, N_EVEN, PAST_LEN, DA_HEADS, 2 * DA_QK_DIM)),
        "cache_da_v": nrm((DEC_BATCH, N_EVEN, PAST_LEN, DA_HEADS, DA_V_DIM)),
        "cache_mla_ckv": nrm((DEC_BATCH, N_EVEN, PAST_LEN, MLA_KV_RANK)),
        "cache_mla_krope": nrm((DEC_BATCH, N_EVEN, PAST_LEN, MLA_ROPE)),
        "cache_gq_k": nrm((DEC_BATCH, N_ODD, PAST_LEN, GQ_KV_HEADS, GQ_DIM)),
        "cache_gq_v": nrm((DEC_BATCH, N_ODD, PAST_LEN, GQ_KV_HEADS, GQ_DIM)),
        "c": nrm((DEC_BATCH, D)),
        "c_ctx": nrm((D,)),
        "norm1_g": gain((DEPTH, D)),
        "norm2_g": gain((DEPTH, D)),
        "ada_w": nrm((DEPTH, D, 6 * D), 0.5 * D ** -0.5),
        "ada_b": nrm((DEPTH, 6 * D), 0.02),
        "ff1_w": nrm((DEPTH, D, D_FF), D ** -0.5),
        "ff2_w": nrm((DEPTH, D_FF, D), D_FF ** -0.5),
        "ab_w_in": nrm((N_EVEN, D, AB_IN), D ** -0.5),
        "ab_w_out": nrm((N_EVEN, AB_OUT, D), AB_OUT ** -0.5),
        "da_lambda_q1": nrm((N_EVEN, DA_QK_DIM), 0.1),
        "da_lambda_k1": nrm((N_EVEN, DA_QK_DIM), 0.1),
        "da_lambda_q2": nrm((N_EVEN, DA_QK_DIM), 0.1),
        "da_lambda_k2": nrm((N_EVEN, DA_QK_DIM), 0.1),
        "da_q_norm": gain((N_EVEN, DA_QK_DIM)),
        "da_k_norm": gain((N_EVEN, DA_QK_DIM)),
        "da_subln": gain((N_EVEN, DA_V_DIM)),
        "mla_q_a_norm": gain((N_EVEN, MLA_Q_RANK)),
        "mla_w_q_up": nrm((N_EVEN, MLA_Q_RANK, MLA_HEADS * MLA_QK), MLA_Q_RANK ** -0.5),
        "mla_kv_a_norm": gain((N_EVEN, MLA_KV_RANK)),
        "mla_w_kv_up": nrm((N_EVEN, MLA_KV_RANK, MLA_HEADS * (MLA_NOPE + MLA_V)), MLA_KV_RANK ** -0.5),
        "mla_q_norm": gain((N_EVEN, MLA_QK)),
        "mla_k_norm": gain((N_EVEN, MLA_QK)),
        "c_w_in": nrm((N_ODD, D, C_IN), D ** -0.5),
        "c_w_out": nrm((N_ODD, C_OUT, D), C_OUT ** -0.5),
        "gq_q_norm": gain((N_ODD, GQ_DIM)),
        "gq_k_norm": gain((N_ODD, GQ_DIM)),
        "gq_sink": nrm((N_ODD, GQ_HEADS), 0.5),
    }


def reference(x_prompt, x_sample, cache_da_k, cache_da_v, cache_mla_ckv, cache_mla_krope,
              cache_gq_k, cache_gq_v, c, c_ctx, norm1_g, norm2_g, ada_w, ada_b, ff1_w, ff2_w,
              ab_w_in, ab_w_out, da_lambda_q1, da_lambda_k1, da_lambda_q2, da_lambda_k2,
              da_q_norm, da_k_norm, da_subln, mla_q_a_norm, mla_w_q_up, mla_kv_a_norm,
              mla_w_kv_up, mla_q_norm, mla_k_norm, c_w_in, c_w_out, gq_q_norm, gq_k_norm,
              gq_sink):
    P = {"norm1_g": norm1_g, "norm2_g": norm2_g, "ada_w": ada_w, "ada_b": ada_b,
         "ff1_w": ff1_w, "ff2_w": ff2_w, "ab_w_in": ab_w_in, "ab_w_out": ab_w_out,
         "da_lambda_q1": da_lambda_q1, "da_lambda_k1": da_lambda_k1,
         "da_lambda_q2": da_lambda_q2, "da_lambda_k2": da_lambda_k2,
         "da_q_norm": da_q_norm, "da_k_norm": da_k_norm, "da_subln": da_subln,
         "mla_q_a_norm": mla_q_a_norm, "mla_w_q_up": mla_w_q_up,
         "mla_kv_a_norm": mla_kv_a_norm, "mla_w_kv_up": mla_w_kv_up,
         "mla_q_norm": mla_q_norm, "mla_k_norm": mla_k_norm,
         "c_w_in": c_w_in, "c_w_out": c_w_out, "gq_q_norm": gq_q_norm,
         "gq_k_norm": gq_k_norm, "gq_sink": gq_sink}
    y_prompt, new = run_trunk(x_prompt, c_ctx, None, None, P)
    s = x_sample.shape[1]
    rows = s // GRID_W
    pos = (jnp.repeat(jnp.arange(rows), GRID_W), jnp.tile(jnp.arange(GRID_W), rows))
    cache = {"da_k": cache_da_k, "da_v": cache_da_v, "mla_ckv": cache_mla_ckv,
             "mla_krope": cache_mla_krope, "gq_k": cache_gq_k, "gq_v": cache_gq_v}
    y_sample, _ = run_trunk(x_sample, c, pos, cache, P)
    new_da_k = jnp.stack(new["da_k"], axis=1)
    new_da_v = jnp.stack(new["da_v"], axis=1)
    new_mla_ckv = jnp.stack(new["mla_ckv"], axis=1)
    new_mla_krope = jnp.stack(new["mla_krope"], axis=1)
    new_gq_k = jnp.stack(new["gq_k"], axis=1)
    new_gq_v = jnp.stack(new["gq_v"], axis=1)
    return (y_prompt, y_sample, new_da_k, new_da_v, new_mla_ckv, new_mla_krope, new_gq_k, new_gq_v)
```

```python
import contextlib
import math
import numpy as np
import concourse.bass as bass
import concourse.mybir as mybir
from concourse.bass_utils import run_bass_kernel_spmd

F32 = mybir.dt.float32
BF16 = mybir.dt.bfloat16
U8 = mybir.dt.uint8
AF = mybir.ActivationFunctionType
ALU = mybir.AluOpType

PE, ACT, DVE, POOL, SP = "pe", "act", "dve", "pool", "sp"
EPS = 1e-6
NEG = -30000.0


class Tile:
    def __init__(self, name, h, nsub=1, psum=False, off=0, nbytes=0, ghosts=()):
        self.name, self.h, self.nsub, self.psum = name, h, nsub, psum
        self.off, self.nbytes = off, nbytes
        self.ghosts = list(ghosts)
        self.summary = None
        self.inherit = None

    def __getitem__(self, idx):
        return self.h[idx]


def _keys(items):
    out = []
    for it in items:
        if isinstance(it, tuple):
            t, sub = it
            if sub is None:
                out.extend((t, s) for s in range(t.nsub))
            elif isinstance(sub, (list, range)):
                out.extend((t, s) for s in sub)
            else:
                out.append((t, sub))
        else:
            out.extend((it, s) for s in range(it.nsub))
    return out


class Op:
    __slots__ = ("eng", "fn", "reads", "writes", "dma", "deps", "has_dep", "token")


class Arena:
    def __init__(self, nc, base, size):
        self.nc, self.base, self.size = nc, base, size
        self.free = [(0, size)]
        self.ghosts = []
        self.n = 0
        self.peak = 0

    def alloc(self, name, shape, dt, nsub=1, top=False):
        esz = {F32: 4, BF16: 2, U8: 1}[dt]
        nb = esz
        for s in shape[1:]:
            nb *= s
        nb = (nb + 31) // 32 * 32
        order = range(len(self.free) - 1, -1, -1) if top else range(len(self.free))
        for i in order:
            s, e = self.free[i]
            if e - s >= nb:
                if e - s == nb:
                    off = s
                    self.free.pop(i)
                elif top:
                    off = e - nb
                    self.free[i] = (s, e - nb)
                else:
                    off = s
                    self.free[i] = (s + nb, e)
                break
        else:
            raise RuntimeError(f"arena OOM allocating {name} {shape} ({nb} B); free={self.free}")
        self.n += 1
        uname = f"{name}_{self.n}"
        h = self.nc.alloc_sbuf_tensor_at(uname, list(shape), dt, offset=self.base + off)
        inh = [g for g in self.ghosts if g[0] < off + nb and g[1] > off]
        self.ghosts = [g for g in self.ghosts if not (g[0] >= off and g[1] <= off + nb)]
        t = Tile(uname, h, nsub=nsub, off=off, nbytes=nb, ghosts=[g[2] for g in inh])
        self.peak = max(self.peak, off + nb)
        return t

    def release(self, t):
        s, e = t.off, t.off + t.nbytes
        self.ghosts.append((s, e, t))
        fr = self.free + [(s, e)]
        fr.sort()
        merged = []
        for a, b in fr:
            if merged and merged[-1][1] == a:
                merged[-1] = (merged[-1][0], b)
            else:
                merged.append((a, b))
        self.free = merged


class Sched:
    def __init__(self, nc, st, n_dma_sems=48):
        self.nc, self.st = nc, st
        self.engs = {PE: nc.tensor, ACT: nc.scalar, DVE: nc.vector, POOL: nc.gpsimd, SP: nc.sync}
        self.ops = []
        self.wloads = []
        self.EPOCH = 3000
        self.n_dma_sems = n_dma_sems

    def op(self, eng, fn, reads=(), writes=(), dma=False, defer_pos=None):
        o = Op()
        o.eng, o.fn, o.dma = eng, fn, dma
        o.reads, o.writes = _keys(reads), _keys(writes)
        o.has_dep, o.token, o.deps = False, None, None
        if defer_pos is None:
            self.ops.append(o)
        else:
            self.wloads.append((defer_pos, o))
        return o

    def pos(self):
        return len(self.ops)

    def _tile_inherit(self, t, ops):
        if t.inherit is not None:
            return t.inherit
        eng_max, dmas = {}, set()
        for g in t.ghosts:
            if g.summary is not None:
                ge, gd = g.summary
            else:
                ge, gd = self._tile_inherit(g, ops)
            for e, i in ge.items():
                if eng_max.get(e, -1) < i:
                    eng_max[e] = i
            dmas |= gd
        t.inherit = (eng_max, dmas)
        return t.inherit

    def finalize(self):
        ins = {}
        for p, o in self.wloads:
            ins.setdefault(p, []).append(o)
        merged = []
        for i, o in enumerate(self.ops):
            if i in ins:
                merged.extend(ins[i])
            merged.append(o)
        if len(self.ops) in ins:
            merged.extend(ins[len(self.ops)])
        ops = merged
        last_w, rd_e, rd_d = {}, {}, {}
        for idx, o in enumerate(ops):
            deps = set()
            rkeys = set(o.reads)
            tiles = {}
            for r in o.reads:
                tiles[r[0].name] = r[0]
                w = last_w.get(r)
                if w is not None:
                    deps.add(w)
                if r[0].psum:
                    deps.update(rd_e.get(r, {}).values())
            for r in o.writes:
                tiles[r[0].name] = r[0]
                w = last_w.get(r)
                if w is not None:
                    deps.add(w)
                deps.update(rd_e.get(r, {}).values())
                deps.update(rd_d.get(r, ()))
            for t in tiles.values():
                if t.ghosts:
                    ge, gd = self._tile_inherit(t, ops)
                    deps.update(ge.values())
                    deps.update(gd)
            fd = []
            for d in deps:
                p = ops[d]
                if p.dma:
                    fd.append(d)
                    continue
                if p.eng == o.eng and not o.dma:
                    if o.eng == PE:
                        continue
                    if not any((k in rkeys) for k in p.writes):
                        continue
                fd.append(d)
            o.deps = fd
            for d in fd:
                ops[d].has_dep = True
            for r in o.reads:
                if o.dma:
                    rd_d.setdefault(r, []).append(idx)
                else:
                    rd_e.setdefault(r, {})[o.eng] = idx
            for r in o.writes:
                last_w[r] = idx
                rd_e[r] = {}
                rd_d[r] = []
            for t in tiles.values():
                if t.summary is None:
                    t.summary = ({}, set())
                if o.dma:
                    t.summary[1].add(idx)
                else:
                    t.summary[0][o.eng] = idx
        self._emit(ops)

    def _emit(self, ops):
        nc, st = self.nc, self.st
        esem = {e: [st.enter_context(nc.semaphore(f"s_{e}0"))] for e in (PE, ACT, DVE, POOL)}
        dsem = [st.enter_context(nc.semaphore(f"d{i}")) for i in range(self.n_dma_sems)]
        cnt = {e: 0 for e in esem}
        ep = {e: 0 for e in esem}
        waited = {e: {} for e in self.engs}
        nd = len(dsem)
        dma_i = 0
        dsem_last = [None] * nd
        dsem_cnt = [0] * nd
        nwaits = 0
        for o in ops:
            e = self.engs[o.eng]
            need = {}
            for d in o.deps:
                key, sem, val = ops[d].token
                if need.get(key, (None, 0))[1] < val:
                    need[key] = (sem, val)
            if o.dma:
                k = dma_i % nd
                dma_i += 1
                if dsem_last[k] is not None:
                    key, sem, val = dsem_last[k]
                    if need.get(key, (None, 0))[1] < val:
                        need[key] = (sem, val)
            for key, (sem, val) in need.items():
                if waited[o.eng].get(key, 0) >= val:
                    continue
                e.wait_ge(sem, val)
                nwaits += 1
                waited[o.eng][key] = val
            ins = o.fn()
            if o.dma:
                dsem_cnt[k] += 16
                ins.then_inc(dsem[k], 16)
                o.token = (("d", k), dsem[k], dsem_cnt[k])
                dsem_last[k] = o.token
            elif o.has_dep:
                if cnt[o.eng] >= self.EPOCH:
                    cnt[o.eng] = 0
                    ep[o.eng] += 1
                    esem[o.eng].append(st.enter_context(nc.semaphore(f"s_{o.eng}{ep[o.eng]}")))
                cnt[o.eng] += 1
                sem = esem[o.eng][-1]
                ins.then_inc(sem, 1)
                o.token = ((o.eng, ep[o.eng]), sem, cnt[o.eng])
        sp = self.engs[SP]
        for k in range(nd):
            if dsem_last[k] is not None:
                key, sem, val = dsem_last[k]
                sp.wait_ge(sem, val)
        self.stats = dict(n_ops=len(ops), n_waits=nwaits, epochs=dict(ep))


N1G, N2G, ADAB = 0, 32, 64
DAQN, DAKN, SUBLN = 448, 449, 450
MQN, MKVN = 451, 455
QNN, QNR, KNN, KNR = 459, 460, 461, 462
GQQN, GQKN = 463, 464
SINK = 465
LAMC = 481
CONDT = 485
NSP = 520
LAM_INIT0 = 0.8 - 0.6 * math.exp(-0.3 * 0)
NSLOT = 4


def build_program(n_pgroups=2, do_sample=True, dbg=None):
    nc = bass.Bass("TRN2", target_bir_lowering=False)
    st = contextlib.ExitStack()
    S = Sched(nc, st)

    def din(name, shape):
        return nc.dram_tensor(name, list(shape), F32, kind="ExternalInput").ap()

    def dout(name, shape):
        return nc.dram_tensor(name, list(shape), F32, kind="ExternalOutput").ap()

    xTp = din("xTp", [2048, 1024])
    xTs = din("xTs", [2048, 1024])
    c_dakT = din("c_dakT", [8, 128, 512])
    c_dav = din("c_dav", [512, 1024])
    c_ckvT = din("c_ckvT", [512, 512])
    c_krT = din("c_krT", [64, 512])
    c_gqkT = din("c_gqkT", [4, 128, 512])
    c_gqv = din("c_gqv", [512, 512])
    spd = din("sp", [128, NSP])
    cmat = din("cmat", [128, 5, 128])
    rope = din("rope", [128, 4, 1024])
    maskd = din("maskl1", [128, 5, 512])
    ada_w = din("ada_w", [2, 2048, 12288])
    ff1_w = din("ff1_w", [2, 2048, 8192])
    ff2_w = din("ff2_w", [2, 8192, 2048])
    ab_w_in = din("ab_w_in", [1, 2048, 4160])
    ab_w_out = din("ab_w_out", [1, 2048, 2048])
    w_q_up = din("w_q_up", [1, 512, 1536])
    w_kv_up = din("w_kv_up", [1, 512, 2048])
    c_w_in = din("c_w_in", [1, 2048, 3072])
    c_w_out = din("c_w_out", [1, 2048, 2048])

    yTp = dout("yTp", [2048, 1024])
    yTs = dout("yTs", [2048, 512])
    o_dak = dout("o_dak", [8, 128, 1024])
    o_dav = dout("o_dav", [1024, 1024])
    o_ckv = dout("o_ckv", [512, 1024])
    o_kr = dout("o_kr", [64, 1024])
    o_gqk = dout("o_gqk", [4, 128, 1024])
    o_gqv = dout("o_gqv", [1024, 512])

    def wview(w):
        return w.rearrange("(kc p) n -> p kc n", p=128)

    ARENA_BYTES = 207 * 1024
    slab = st.enter_context(nc.sbuf_tensor("arena", [128, ARENA_BYTES], U8))
    base = nc.lookup_mloc(slab).addr
    AR = Arena(nc, base, ARENA_BYTES)

    PSB = []
    for b in range(8):
        h = st.enter_context(nc.psum_tensor(f"ps{b}", [128, 512], F32))
        PSB.append(Tile(f"ps{b}", h, psum=True))

    class Rot:
        def __init__(self, banks):
            self.banks, self.i = banks, 0

        def next(self):
            b = self.banks[self.i % len(self.banks)]
            self.i += 1
            return b

    slots = []
    for i in range(NSLOT):
        t = AR.alloc(f"wslot{i}", [128, 16, 512], BF16)
        t.views = {(16, 512): t.h}
        slots.append(t)

    def slot_view(t, nkc, ncols):
        key = (nkc, ncols)
        if key not in t.views:
            AR.n += 1
            t.views[key] = nc.alloc_sbuf_tensor_at(f"{t.name}_v{AR.n}", [128, nkc, ncols], BF16,
                                                   offset=AR.base + t.off)
        return t.views[key]

    wstate = dict(j=0, reqpos=[])

    def wnext(view, kc0, nkc, c0, ncols):
        j = wstate["j"]
        wstate["j"] += 1
        t = slots[j % NSLOT]
        assert not wstate["reqpos"] or S.pos() > wstate["reqpos"][-1], "ring tiles must be requested in use order"
        wstate["reqpos"].append(S.pos())
        pos = wstate["reqpos"][max(0, j - NSLOT + 1)]
        dst = slot_view(t, nkc, ncols)
        src = view[:, kc0:kc0 + nkc, c0:c0 + ncols]
        S.op(POOL, lambda: nc.gpsimd.dma_start(out=dst[:], in_=src), writes=[t], dma=True, defer_pos=pos)
        return t, dst

    def mm(bank, out_ap, pairs, reads, start=True, stop=True):
        pairs = list(pairs)

        def fn():
            ins = None
            n = len(pairs)
            for i, (l, r) in enumerate(pairs):
                ins = nc.tensor.matmul(out_ap, lhsT=l, rhs=r, start=(start and i == 0), stop=(stop and i == n - 1))
            return ins
        S.op(PE, fn, reads=reads, writes=[bank])

    def act(out, in_, func, reads, writes, scale=1.0, bias=None):
        if bias is None:
            S.op(ACT, lambda: nc.scalar.activation(out=out, in_=in_, func=func, scale=scale), reads=reads, writes=writes)
        else:
            S.op(ACT, lambda: nc.scalar.activation(out=out, in_=in_, func=func, scale=scale, bias=bias), reads=reads, writes=writes)

    def tt(eng, out, a, b, op, reads, writes):
        e = nc.vector if eng == DVE else nc.gpsimd
        S.op(eng, lambda: e.tensor_tensor(out=out, in0=a, in1=b, op=op), reads=reads, writes=writes)

    def ts(out, a, s1, s2, op0, op1, reads, writes):
        if s2 is None:
            S.op(DVE, lambda: nc.vector.tensor_scalar(out=out, in0=a, scalar1=s1, scalar2=None, op0=op0), reads=reads, writes=writes)
        else:
            S.op(DVE, lambda: nc.vector.tensor_scalar(out=out, in0=a, scalar1=s1, scalar2=s2, op0=op0, op1=op1), reads=reads, writes=writes)

    def stt(out, in0, scalar, in1, op0, op1, reads, writes):
        S.op(DVE, lambda: nc.vector.scalar_tensor_tensor(out=out, in0=in0, scalar=scalar, in1=in1, op0=op0, op1=op1), reads=reads, writes=writes)

    def cpy(eng, out, in_, reads, writes):
        if eng == ACT:
            S.op(ACT, lambda: nc.scalar.activation(out=out, in_=in_, func=AF.Copy), reads=reads, writes=writes)
        elif eng == DVE:
            S.op(DVE, lambda: nc.vector.tensor_copy(out=out, in_=in_), reads=reads, writes=writes)
        else:
            S.op(POOL, lambda: nc.gpsimd.tensor_copy(out=out, in_=in_), reads=reads, writes=writes)

    def dma(eng, out, in_, reads, writes):
        e = nc.sync if eng == SP else nc.gpsimd
        S.op(eng, lambda: e.dma_start(out=out, in_=in_), reads=reads, writes=writes, dma=True)

    spt = AR.alloc("sp", [128, NSP], F32)
    dma(SP, spt[:], spd, [], [spt])
    cm = AR.alloc("cm", [128, 5, 128], BF16)
    dma(POOL, cm[:], cmat, [], [cm])
    cm32 = AR.alloc("cm32", [128, 2, 128], F32)
    dma(SP, cm32[:, 0, :], cmat[:, 0, :], [], [cm32])
    dma(SP, cm32[:, 1, :], cmat[:, 2, :], [], [cm32])
    ONES, BONES, IDENT, RDA, RGQ = (cm[:, i, :] for i in range(5))
    epsc = AR.alloc("epsc", [128, 1], F32)
    S.op(DVE, lambda: nc.vector.memset(epsc[:], EPS), writes=[epsc])

    dv = AR.alloc("dv", [128, 2 * 2 * 6 * 16 + 64], F32)

    def DV(l, j, kind):
        o = ((l * 2 + j) * 6 + kind) * 16
        return dv.h[:, o:o + 16]
    DVX = 2 * 2 * 6 * 16
    ES = dv.h[:, DVX:DVX + 16]
    NEGLAM = dv.h[:, DVX + 16:DVX + 17]
    SUBLN_EFF = dv.h[:, DVX + 17:DVX + 18]

    PR = Rot(PSB[0:5])
    PX = Rot(PSB[5:8])

    silu = AR.alloc("silu", [128, 32], BF16)
    act(silu[:], spt[:, CONDT:CONDT + 32], AF.Silu, [spt], [silu])
    mod = [AR.alloc(f"mod{l}", [128, 192], F32, nsub=24) for l in range(2)]
    mtoks = [AR.alloc(f"mtok{i}", [2, 512], F32) for i in range(2)]
    dv.nsub = 27

    def DVK(l, j, kind):
        return (dv, (l * 2 + j) * 6 + kind)
    K_ES, K_NEGLAM, K_SUBLN = (dv, 24), (dv, 25), (dv, 26)
    ada_state = dict(next=0)

    def derive(l, what):
        m3 = mod[l].h[:].rearrange("p (c j) -> p c j", j=2)
        n1 = spt[:, N1G + l * 16:N1G + l * 16 + 16]
        n2 = spt[:, N2G + l * 16:N2G + l * 16 + 16]
        for j in range(2):
            if what == 0:
                stt(DV(l, j, 0), m3[:, 16:32, j], 1.0, n1, ALU.add, ALU.mult, [(mod[l], range(4, 8)), spt], [DVK(l, j, 0)])
                cpy(DVE, DV(l, j, 1), m3[:, 0:16, j], [(mod[l], range(0, 4))], [DVK(l, j, 1)])
            elif what == 1:
                cpy(DVE, DV(l, j, 2), m3[:, 32:48, j], [(mod[l], range(8, 12))], [DVK(l, j, 2)])
            elif what == 2:
                stt(DV(l, j, 3), m3[:, 64:80, j], 1.0, n2, ALU.add, ALU.mult, [(mod[l], range(16, 20)), spt], [DVK(l, j, 3)])
                cpy(DVE, DV(l, j, 4), m3[:, 48:64, j], [(mod[l], range(12, 16))], [DVK(l, j, 4)])
            else:
                cpy(DVE, DV(l, j, 5), m3[:, 80:96, j], [(mod[l], range(20, 24))], [DVK(l, j, 5)])

    def ada_step(rot_a, rot_b):
        k = ada_state["next"]
        if k >= 48:
            return False
        ada_state["next"] += 1
        l, j4 = divmod(k, 24)
        vw = wview(ada_w[l])
        wt, wv = wnext(vw, 0, 16, j4 * 512, 512)
        bank = rot_a.next()
        mm(bank, bank[0:2, :], [(silu[:, kc * 2:kc * 2 + 2], wv[:, kc, :]) for kc in range(16)], [silu, wt])
        mt = mtoks[k % 2]
        cpy(ACT, mt[:], bank[0:2, :], [bank], [mt])
        bt = rot_b.next()

        def fn(bt=bt, mt=mt):
            ins = None
            for i in range(4):
                ins = nc.tensor.transpose(bt[:, 2 * i:2 * i + 2], mt[0:2, 128 * i:128 * i + 128], cm32[0:2, 1, 0:2])
            return ins
        S.op(PE, fn, reads=[mt, cm32], writes=[bt])
        c0 = ADAB + l * 192 + j4 * 8
        tt(DVE, mod[l][:, j4 * 8:j4 * 8 + 8], bt[:, 0:8], spt[:, c0:c0 + 8], ALU.add, [bt, spt], [(mod[l], j4)])
        if j4 in (7, 11, 19, 23):
            derive(l, {7: 0, 11: 1, 19: 2, 23: 3}[j4])
        return True

    for _ in range(8):
        ada_step(PR, PX)
    act(ES, spt[:, SINK:SINK + 16], AF.Exp, [spt], [K_ES])
    lamp = AR.alloc("lamp", [128, 2], F32)
    S.op(DVE, lambda: nc.vector.memset(lamp[:], 0.0), writes=[lamp])
    tt(DVE, lamp[0:64, 0:1], spt[0:64, LAMC:LAMC + 1], spt[0:64, LAMC + 1:LAMC + 2], ALU.mult, [spt, lamp], [lamp])
    tt(DVE, lamp[0:64, 1:2], spt[0:64, LAMC + 2:LAMC + 3], spt[0:64, LAMC + 3:LAMC + 4], ALU.mult, [spt, lamp], [lamp])
    bl = PX.next()
    mm(bl, bl[:, 0:2], [(cm32[0:64, 0, :], lamp[0:64, :])], [cm32, lamp])
    lame = AR.alloc("lame", [128, 2], F32)
    act(lame[:], bl[:, 0:2], AF.Exp, [bl], [lame])
    stt(NEGLAM, lame[:, 1:2], -LAM_INIT0, lame[:, 0:1], ALU.add, ALU.subtract, [lame], [K_NEGLAM])
    ts(SUBLN_EFF, spt[:, SUBLN:SUBLN + 1], 1.0 - LAM_INIT0, None, ALU.mult, ALU.bypass, [spt], [K_SUBLN])

    dbg_outs = {}

    def dump(name, ap, shape, reads, bf16=False):
        if dbg is None or name not in dbg or name in dbg_outs:
            return
        o = dout("dbg_" + name, shape)
        dbg_outs[name] = o
        dma(POOL if bf16 else SP, o, ap, reads, [])
    ctx = dict(dbg_on=dbg is not None, dump=dump, nc=nc, S=S, AR=AR, PSB=PSB, PR=PR, PX=PX, mm=mm, act=act, tt=tt, ts=ts,
               stt=stt, cpy=cpy, dma=dma, wnext=wnext, wview=wview, spt=spt, ONES=ONES, BONES=BONES, IDENT=IDENT,
               RDA=RDA, RGQ=RGQ, cm=cm, epsc=epsc, DV=DV, DVK=DVK, dv=dv, ES=ES, NEGLAM=NEGLAM, SUBLN_EFF=SUBLN_EFF,
               K_ES=K_ES, K_NEGLAM=K_NEGLAM, K_SUBLN=K_SUBLN, Rot=Rot, ada_step=ada_step)
    d = dict(xTp=xTp, xTs=xTs, c_dakT=c_dakT, c_dav=c_dav, c_ckvT=c_ckvT, c_krT=c_krT, c_gqkT=c_gqkT, c_gqv=c_gqv,
             rope=rope, maskd=maskd, ff1_w=ff1_w, ff2_w=ff2_w, ab_w_in=ab_w_in, ab_w_out=ab_w_out, w_q_up=w_q_up,
             w_kv_up=w_kv_up, c_w_in=c_w_in, c_w_out=c_w_out, yTp=yTp, yTs=yTs, o_dak=o_dak, o_dav=o_dav,
             o_ckv=o_ckv, o_kr=o_kr, o_gqk=o_gqk, o_gqv=o_gqv)
    first = True

    def ada_done():
        while ada_step(PR, PX):
            pass
        for t in [silu, lamp, lame] + mod + mtoks:
            AR.release(t)
    for g in range(n_pgroups):
        run_group(ctx, d, kind="P", g=g, first=first)
        if first:
            ada_done()
        first = False
    if do_sample:
        run_group(ctx, d, kind="S", g=0, first=first)
    S.finalize()
    st.close()
    return nc, S, AR


def ntiles(T):
    out, t = [], 0
    while t < T:
        n = min(512, T - t)
        out.append((t, n))
        t += n
    return out


def pipeline(gens):
    active = []

    def step_all():
        for a in list(active):
            try:
                next(a)
            except StopIteration:
                active.remove(a)
    for g in gens:
        active.append(g)
        step_all()
    while active:
        step_all()


def run_group(ctx, d, kind, g, first=False):
    nc, S, AR, PSB, PR, PX = ctx["nc"], ctx["S"], ctx["AR"], ctx["PSB"], ctx["PR"], ctx["PX"]
    mm, act, tt, ts, stt, cpy, dma = ctx["mm"], ctx["act"], ctx["tt"], ctx["ts"], ctx["stt"], ctx["cpy"], ctx["dma"]
    wnext, wview, spt, DV, DVK, dv = ctx["wnext"], ctx["wview"], ctx["spt"], ctx["DV"], ctx["DVK"], ctx["dv"]
    ONES, BONES, IDENT, RDA, RGQ, cm, epsc = ctx["ONES"], ctx["BONES"], ctx["IDENT"], ctx["RDA"], ctx["RGQ"], ctx["cm"], ctx["epsc"]
    ES, NEGLAM, SUBLN_EFF, Rot = ctx["ES"], ctx["NEGLAM"], ctx["SUBLN_EFF"], ctx["Rot"]
    K_ES, K_NEGLAM, K_SUBLN = ctx["K_ES"], ctx["K_NEGLAM"], ctx["K_SUBLN"]
    P = kind == "P"
    cj = 0 if P else 1
    if P:
        TQ0, TKV0, TQ1, TKV1, KC0, KC1 = 512, 512, 512, 512, 0, 0
        xT = d["xTp"].rearrange("(c p) t -> p c t", p=128)[:, :, g * 512:(g + 1) * 512]
        yT = d["yTp"].rearrange("(c p) t -> p c t", p=128)[:, :, g * 512:(g + 1) * 512]
        tok0 = g * 512
    else:
        TQ0, TKV0, TQ1, TKV1, KC0, KC1 = 640, 1024, 512, 640, 512, 512
        xT = d["xTs"].rearrange("(c p) t -> p c t", p=128)
        yT = d["yTs"].rearrange("(c p) t -> p c t", p=128)
        tok0 = 0
    TQM = max(TQ0, TQ1)
    SB4 = Rot(PSB[0:4])

    def ada_hook(n):
        if first:
            for _ in range(n):
                ctx["ada_step"](SB4, SB4)

    class Pool_:
        def __init__(self, name, shape, dt, n):
            self.t = [AR.alloc(f"{name}{i}", shape, dt) for i in range(n)]
            self.i = 0

        def next(self):
            t = self.t[self.i % len(self.t)]
            self.i += 1
            return t

        def free(self):
            for t in self.t:
                AR.release(t)

    sqp = Pool_("sq", [128, 512], BF16, 5)
    f32p = Pool_("f32s", [128, 512], F32, 4)
    ep = Pool_("E", [128, 512], BF16, 5)
    lnp = Pool_("lnp", [128, 512], F32, 2)
    rsp = Pool_("rsp", [128, 512], F32, 2)
    o32p = Pool_("o32", [128, 512], F32, 3 if P else 0)

    def rstd_from(bank, n, D, np_=128):
        l = lnp.next()
        act(l[0:np_, 0:n], bank[0:np_, 0:n], AF.Ln, [bank, epsc], [l], scale=1.0 / D, bias=epsc[0:np_, :])
        r = rsp.next()
        act(r[0:np_, 0:n], l[0:np_, 0:n], AF.Exp, [l], [r], scale=-0.5)
        return r

    def rms_mod(xsrc, tiles, A, B, KA, KB, h):
        for ti, (t0, n) in enumerate(tiles):
            xt, xap = xsrc(ti, t0, n)
            sb = PX.next()
            for c in range(16):
                sq = sqp.next()
                act(sq[:, 0:n], xap(c), AF.Square, [xt], [sq])
                mm(sb, sb[:, 0:n], [(ONES, sq[:, 0:n])], [cm, sq], start=(c == 0), stop=(c == 15))
            r = rstd_from(sb, n, 2048.0)
            for c in range(16):
                tmp = f32p.next()
                stt(tmp[:, 0:n], xap(c), A[:, c:c + 1], r[:, 0:n], ALU.mult, ALU.mult, [xt, KA, r], [tmp])
                act(h[:, c, t0:t0 + n], tmp[:, 0:n], AF.Identity, [tmp, KB], [(h, ti)], scale=1.0, bias=B[:, c:c + 1])

    ropet = None
    if not P:
        ropet = AR.alloc("ropet", [128, 2, 1024], F32)
        for i in range(2):
            dma(SP, ropet[:, i, :], d["rope"][:, i, :], [], [ropet])

    def rope_tail(y, np_, n, Rm, cosap, sinap, out_ap, out_t):
        rb = PX.next()
        mm(rb, rb[0:np_, 0:n], [(Rm, y[0:np_, 0:n])], [cm, y])
        yield
        t1 = f32p.next()
        tt(DVE, t1[0:np_, 0:n], y[0:np_, 0:n], cosap, ALU.mult, [y, ropet], [t1])
        t2 = f32p.next()
        tt(DVE, t2[0:np_, 0:n], rb[0:np_, 0:n], sinap, ALU.mult, [rb, ropet], [t2])
        tt(DVE, out_ap, t1[0:np_, 0:n], t2[0:np_, 0:n], ALU.add, [t1, t2], [out_t])

    def qk_unit(proj_pairs, proj_reads, n, ss_fn, D, gain, out_ap, out_t, rope=None, out32=None, after=None):
        bank = PR.next()
        mm(bank, bank[:, 0:n], proj_pairs, proj_reads)
        yield
        sq = sqp.next()
        act(sq[:, 0:n], bank[:, 0:n], AF.Square, [bank], [sq])
        sb = PX.next()
        pairs, rds = ss_fn(sq)
        mm(sb, sb[:, 0:n], pairs, [cm, sq] + list(rds))
        yield
        r = rstd_from(sb, n, D)
        if after is not None:
            after(r)
        if rope is None:
            if out32 is None:
                stt(out_ap, bank[:, 0:n], gain, r[:, 0:n], ALU.mult, ALU.mult, [bank, spt, r], [out_t])
            else:
                o32t = o32p.next()
                stt(o32t[:, 0:n], bank[:, 0:n], gain, r[:, 0:n], ALU.mult, ALU.mult, [bank, spt, r], [o32t])
                cpy(ACT, out_ap, o32t[:, 0:n], [o32t], [out_t])
                dma(SP, out32, o32t[:, 0:n], [o32t], [])
        else:
            Rm, cosap, sinap = rope
            y = sqp.next()
            stt(y[:, 0:n], bank[:, 0:n], gain, r[:, 0:n], ALU.mult, ALU.mult, [bank, spt, r], [y])
            yield from rope_tail(y, 128, n, Rm, cosap, sinap, out_ap, out_t)

    def v_unit(pairs, reads, out_ap, out_t, out32=None):
        bank = PR.next()
        mm(bank, bank[:, :], pairs, reads)
        yield
        cpy(ACT, out_ap, bank[:, :], [bank], [out_t])
        if out32 is not None:
            o32t = o32p.next()
            cpy(DVE, o32t[:, :], bank[:, :], [bank], [o32t])
            dma(SP, out32, o32t[:, :], [o32t], [])

    ACC = [(PSB[4], PSB[5]), (PSB[6], PSB[7])]
    astate = dict(u=0, pending=[])

    def attn_unit(q0, n, keyts, score_fn, v_fn, fin_a, fin_b=None, lag=2):
        Ob, Db = ACC[astate["u"] % 2]
        astate["u"] += 1
        nk = len(keyts)
        es = {}
        for step in range(nk + lag):
            if step < nk:
                kt = keyts[step]
                sbk = SB4.next()
                pairs, rds, scale = score_fn(kt, q0, n)
                mm(sbk, sbk[:, 0:n], pairs, rds)
                e = ep.next()
                act(e[:, 0:n], sbk[:, 0:n], AF.Exp, [sbk], [e], scale=scale)
                es[step] = e
            j = step - lag
            if j >= 0:
                e = es.pop(j)
                vap, vr = v_fn(keyts[j])
                mm(Ob, Ob[:, 0:n], [(vap, e[:, 0:n])], [e] + list(vr), start=(j == 0), stop=(j == nk - 1))
                mm(Db, Db[:, 0:n], [(ONES, e[:, 0:n])], [e, cm], start=(j == 0), stop=(j == nk - 1))
        for f in astate["pending"]:
            f()
        astate["pending"] = []
        fin_a(q0, n, Ob, Db)
        if fin_b is not None:
            astate["pending"].append(fin_b)

    def attn_flush():
        for f in astate["pending"]:
            f()
        astate["pending"] = []

    def ffn(l, xres, TQ):
        A2, B2, G2 = DV(l, cj, 3), DV(l, cj, 4), DV(l, cj, 5)
        tl = ntiles(TQ)
        h2 = AR.alloc("h2", [128, 16, TQ], BF16, nsub=len(tl), top=True)
        rms_mod(lambda ti, t0, n: (xres, lambda c: xres[:, c, t0:t0 + n]), tl, A2, B2, DVK(l, cj, 3), DVK(l, cj, 4), h2)
        v1 = wview(d["ff1_w"][l])
        v2 = wview(d["ff2_w"][l])
        nsplit = 4 if TQ > 512 else 2
        nw1 = 16 // nsplit
        kcs = 64 // nsplit
        ncol2 = 8192 // kcs
        for half in range(nsplit):
            h1 = AR.alloc("h1", [128, kcs, TQ], BF16, nsub=len(tl))
            for jw in range(nw1):
                wt, wv = wnext(v1, 0, 16, (half * nw1 + jw) * 512, 512)
                for fc in range(4):
                    for ti, (t0, n) in enumerate(tl):
                        bank = PR.next()
                        mm(bank, bank[:, 0:n], [(wv[:, kc, fc * 128:(fc + 1) * 128], h2[:, kc, t0:t0 + n]) for kc in range(16)],
                           [wt, (h2, ti)])
                        r = sqp.next()
                        act(r[:, 0:n], bank[:, 0:n], AF.Relu, [bank], [r])
                        tt(DVE, h1[:, jw * 4 + fc, t0:t0 + n], r[:, 0:n], r[:, 0:n], ALU.mult, [r], [(h1, ti)])
            for jd in range(2048 // ncol2):
                wt, wv = wnext(v2, half * kcs, kcs, jd * ncol2, ncol2)
                for dc in range(ncol2 // 128):
                    ch = jd * (ncol2 // 128) + dc
                    for ti, (t0, n) in enumerate(tl):
                        bank = PR.next()
                        mm(bank, bank[:, 0:n], [(wv[:, kc, dc * 128:(dc + 1) * 128], h1[:, kc, t0:t0 + n]) for kc in range(kcs)],
                           [wt, (h1, ti)])
                        stt(xres[:, ch, t0:t0 + n], bank[:, 0:n], G2[:, ch:ch + 1], xres[:, ch, t0:t0 + n], ALU.mult, ALU.add,
                            [bank, DVK(l, cj, 5), (xres, ch)], [(xres, ch)])
            AR.release(h1)
        AR.release(h2)

    def outproj(wd, G, KG, oT, xres, TQ):
        vo = wview(wd)
        tl = ntiles(TQ)
        for jw in range(4):
            wt, wv = wnext(vo, 0, 16, jw * 512, 512)
            for dc in range(4):
                ch = jw * 4 + dc
                for ti, (t0, n) in enumerate(tl):
                    bank = PR.next()
                    mm(bank, bank[:, 0:n], [(wv[:, c, dc * 128:(dc + 1) * 128], oT[:, c, t0:t0 + n]) for c in range(16)], [wt, oT])
                    stt(xres[:, ch, t0:t0 + n], bank[:, 0:n], G[:, ch:ch + 1], xres[:, ch, t0:t0 + n], ALU.mult, ALU.add,
                        [bank, KG, (xres, ch)], [(xres, ch)])

    A1, B1, G1 = DV(0, cj, 0), DV(0, cj, 1), DV(0, cj, 2)
    xres = AR.alloc("xres", [128, 16, TQM], F32, nsub=16, top=True) if P else None
    tl_kv = ntiles(TKV0)
    tl_q = ntiles(TQ0)
    h = AR.alloc("h", [128, 16, TKV0], BF16, nsub=len(tl_kv))
    if P:
        for c4 in range(4):
            dma(SP, xres[:, c4 * 4:c4 * 4 + 4, 0:512], xT[:, c4 * 4:c4 * 4 + 4, :], [], [(xres, range(c4 * 4, c4 * 4 + 4))])
        rms_mod(lambda ti, t0, n: (xres, lambda c: xres[:, c, t0:t0 + n]), tl_kv, A1, B1, DVK(0, cj, 0), DVK(0, cj, 1), h)
    else:
        xst = AR.alloc("xst", [128, 16, 512], F32)

        def xsrc(ti, t0, n):
            for c4 in range(4):
                dma(SP, xst[:, c4 * 4:c4 * 4 + 4, 0:n], xT[:, c4 * 4:c4 * 4 + 4, t0:t0 + n], [], [xst])
            return xst, (lambda c: xst[:, c, 0:n])
        rms_mod(xsrc, tl_kv, A1, B1, DVK(0, cj, 0), DVK(0, cj, 1), h)
        AR.release(xst)
    ctx["dump"]("h", h[:], [128, 16, TKV0], [h], bf16=True)
    oT = AR.alloc("oT", [128, 16, TQ0], BF16)
    vin = wview(d["ab_w_in"][0])
    vqu = wview(d["w_q_up"][0])
    vkvu = wview(d["w_kv_up"][0])

    NKT = (KC0 + TKV0) // 128
    cosd, sind = (None, None) if P else (ropet[:, 0, :], ropet[:, 1, :])
    dat = [AR.alloc(f"dat{i}", [128, 512], F32) for i in range(4)]
    for hp in range(2):
        qT = AR.alloc("qTda", [128, 4, TQ0], BF16)
        kT = AR.alloc("kTda", [128, 4, KC0 + TKV0], BF16)
        V = AR.alloc("Vda", [128, NKT, 512], BF16)
        if not P:
            for hh in range(4):
                dma(POOL, kT[:, hh, 0:512], d["c_dakT"][hp * 4 + hh], [], [kT])
            for t4 in range(4):
                dma(POOL, V[:, t4, :], d["c_dav"][t4 * 128:(t4 + 1) * 128, hp * 512:(hp + 1) * 512], [], [V])
        units = []
        wt, wv = wnext(vin, 0, 16, hp * 512, 512)
        for hh in range(4):
            for ti, (t0, n) in enumerate(tl_q):
                rp = None if P else (RDA, cosd[:, t0:t0 + n], sind[:, t0:t0 + n])
                units.append(qk_unit([(wv[:, kc, hh * 128:(hh + 1) * 128], h[:, kc, t0:t0 + n]) for kc in range(16)], [wt, (h, ti)],
                                     n, lambda sq, n=n: ([(BONES, sq[:, 0:n])], []), 64.0, spt[:, DAQN:DAQN + 1],
                                     qT[:, hh, t0:t0 + n], qT, rope=rp))
        pipeline(units)
        units = []
        wt, wv = wnext(vin, 0, 16, 1024 + hp * 512, 512)
        for hh in range(4):
            for ti, (t0, n) in enumerate(tl_kv):
                rp = None if P else (RDA, cosd[:, t0:t0 + n], sind[:, t0:t0 + n])
                o32 = d["o_dak"][hp * 4 + hh, :, tok0 + t0:tok0 + t0 + n] if P else None
                units.append(qk_unit([(wv[:, kc, hh * 128:(hh + 1) * 128], h[:, kc, t0:t0 + n]) for kc in range(16)], [wt, (h, ti)],
                                     n, lambda sq, n=n: ([(BONES, sq[:, 0:n])], []), 64.0, spt[:, DAKN:DAKN + 1],
                                     kT[:, hh, KC0 + t0:KC0 + t0 + n], kT, rope=rp, out32=o32))
        pipeline(units)
        units = []
        wt, wv = wnext(vin, 0, 16, 2048 + hp * 512, 512)
        for t4 in range(TKV0 // 128):
            o32 = d["o_dav"][tok0 + t4 * 128:tok0 + (t4 + 1) * 128, hp * 512:(hp + 1) * 512] if P else None
            units.append(v_unit([(h[:, kc, t4 * 128:(t4 + 1) * 128], wv[:, kc, :]) for kc in range(16)], [wt, (h, (t4 * 128) // 512)],
                                V[:, KC0 // 128 + t4, :], V, out32=o32))
        pipeline(units)
        for hh in range(4):
            head = hp * 4 + hh
            if P:
                qtl = [(s * 256, 256, [2 * s, 2 * s + 1]) for s in range(2)]
            else:
                qtl = [(t0, n, list(range(NKT))) for (t0, n) in tl_q]
            for qi, (q0, n, keyts) in enumerate(qtl):
                d1 = dat[(hh * 2 + qi) % 4]
                for si in range(2):
                    lo = 64 * si

                    def sc_(kt, q0, n, hh=hh, lo=lo, kT=kT, qT=qT):
                        return ([(kT[lo:lo + 64, hh, kt * 128:(kt + 1) * 128], qT[lo:lo + 64, hh, q0:q0 + n])], [kT, qT], 64.0 ** -0.5)

                    def vf_(kt, hh=hh, V=V):
                        return V[:, kt, hh * 128:(hh + 1) * 128], [V]
                    if si == 0:
                        def fa_(q0, n, Ob, Db, d1=d1):
                            r1 = f32p.next()
                            S.op(DVE, lambda: nc.vector.reciprocal(out=r1[:, 0:n], in_=Db[:, 0:n]), reads=[Db], writes=[r1])
                            tt(DVE, d1[:, 0:n], Ob[:, 0:n], r1[:, 0:n], ALU.mult, [Ob, r1], [d1])
                        attn_unit(q0, n, keyts, sc_, vf_, fa_)
                    else:
                        def fa_(q0, n, Ob, Db, d1=d1):
                            r2 = f32p.next()
                            S.op(DVE, lambda: nc.vector.reciprocal(out=r2[:, 0:n], in_=Db[:, 0:n]), reads=[Db], writes=[r2])
                            tt(DVE, r2[:, 0:n], Ob[:, 0:n], r2[:, 0:n], ALU.mult, [Ob, r2], [r2])
                            stt(d1[:, 0:n], r2[:, 0:n], NEGLAM, d1[:, 0:n], ALU.mult, ALU.add, [r2, K_NEGLAM, d1], [d1])

                        def fb_(q0=q0, n=n, head=head, d1=d1):
                            sq = sqp.next()
                            act(sq[:, 0:n], d1[:, 0:n], AF.Square, [d1], [sq])
                            sb = SB4.next()
                            mm(sb, sb[:, 0:n], [(ONES, sq[:, 0:n])], [cm, sq])
                            r = rstd_from(sb, n, 128.0)
                            stt(oT[:, head, q0:q0 + n], d1[:, 0:n], SUBLN_EFF, r[:, 0:n], ALU.mult, ALU.mult, [d1, K_SUBLN, r], [oT])
                        attn_unit(q0, n, keyts, sc_, vf_, fa_, fb_)
            ada_hook(2)
        attn_flush()
        AR.release(qT)
        AR.release(kT)
        AR.release(V)
    for t in dat:
        AR.release(t)

    NK = KC0 + TKV0
    mqn = AR.alloc("mqn", [128, 4, TQ0], BF16)
    ckvT = AR.alloc("ckvT", [128, 4, NK], BF16)
    krgr = AR.alloc("krgr", [64, NK], F32)
    sqkr = AR.alloc("sqkr", [64, NK], BF16)
    if not P:
        dma(POOL, ckvT[:, :, 0:512], d["c_ckvT"].rearrange("(c p) t -> p c t", p=128), [], [ckvT])
        krc = AR.alloc("krc", [64, 512], F32)
        dma(SP, krc[:], d["c_krT"], [], [krc])
        act(sqkr[:, 0:512], krc[:], AF.Square, [krc], [sqkr])
        ts(krgr[:, 0:512], krc[:], spt[0:64, KNR:KNR + 1], None, ALU.mult, ALU.bypass, [krc, spt], [krgr])
        AR.release(krc)

    def norm4(col0, T, gcol, out_t, out_fn, out32_dram=None):
        wt, wv = wnext(vin, 0, 16, col0, 512)
        for ti, (t0, n) in enumerate(ntiles(T)):
            banks = [PR.next() for _ in range(4)]
            sb = PX.next()
            for c in range(4):
                mm(banks[c], banks[c][:, 0:n], [(wv[:, kc, c * 128:(c + 1) * 128], h[:, kc, t0:t0 + n]) for kc in range(16)], [wt, (h, ti)])
            for c in range(4):
                sq = sqp.next()
                act(sq[:, 0:n], banks[c][:, 0:n], AF.Square, [banks[c]], [sq])
                mm(sb, sb[:, 0:n], [(ONES, sq[:, 0:n])], [cm, sq], start=(c == 0), stop=(c == 3))
            r = rstd_from(sb, n, 512.0)
            for c in range(4):
                if out32_dram is None:
                    stt(out_fn(c, t0, n), banks[c][:, 0:n], spt[:, gcol + c:gcol + c + 1], r[:, 0:n], ALU.mult, ALU.mult,
                        [banks[c], spt, r], [out_t])
                else:
                    o32t = o32p.next()
                    stt(o32t[:, 0:n], banks[c][:, 0:n], spt[:, gcol + c:gcol + c + 1], r[:, 0:n], ALU.mult, ALU.mult,
                        [banks[c], spt, r], [o32t])
                    cpy(ACT, out_fn(c, t0, n), o32t[:, 0:n], [o32t], [out_t])
                    dma(SP, out32_dram(c, t0, n), o32t[:, 0:n], [o32t], [])
    norm4(3072, TQ0, MQN, mqn, lambda c, t0, n: mqn[:, c, t0:t0 + n])
    norm4(3584, TKV0, MKVN, ckvT, lambda c, t0, n: ckvT[:, c, KC0 + t0:KC0 + t0 + n],
          out32_dram=(lambda c, t0, n: d["o_ckv"][c * 128:(c + 1) * 128, tok0 + t0:tok0 + t0 + n]) if P else None)
    wt, wv = wnext(vin, 0, 16, 4096, 64)

    def kr_unit(ti, t0, n):
        bank = PR.next()
        mm(bank, bank[0:64, 0:n], [(wv[:, kc, 0:64], h[:, kc, t0:t0 + n]) for kc in range(16)], [wt, (h, ti)])
        yield
        act(sqkr[:, KC0 + t0:KC0 + t0 + n], bank[0:64, 0:n], AF.Square, [bank], [sqkr])
        if P:
            o32t = o32p.next()
            cpy(DVE, o32t[0:64, 0:n], bank[0:64, 0:n], [bank], [o32t])
            dma(SP, d["o_kr"][:, tok0 + t0:tok0 + t0 + n], o32t[0:64, 0:n], [o32t], [])
            ts(krgr[:, t0:t0 + n], bank[0:64, 0:n], spt[0:64, KNR:KNR + 1], None, ALU.mult, ALU.bypass, [bank, spt], [krgr])
        else:
            y = sqp.next()
            ts(y[0:64, 0:n], bank[0:64, 0:n], spt[0:64, KNR:KNR + 1], None, ALU.mult, ALU.bypass, [bank, spt], [y])
            yield from rope_tail(y, 64, n, cm[0:64, 3, 0:64], cosd[0:64, t0:t0 + n], sind[0:64, t0:t0 + n],
                                 krgr[:, KC0 + t0:KC0 + t0 + n], krgr)
    pipeline([kr_unit(ti, t0, n) for ti, (t0, n) in enumerate(tl_kv)])
    AR.release(h)

    tl_k = ntiles(NK)
    hb = [dict(kn=AR.alloc(f"kn{i}", [128, NK], BF16), krh=AR.alloc(f"krh{i}", [64, NK], BF16),
               vh=AR.alloc(f"vh{i}", [128, NK], BF16), qn=AR.alloc(f"qn{i}", [128, TQ0], BF16),
               qr=AR.alloc(f"qr{i}", [64, TQ0], BF16)) for i in range(2)]

    def mla_q_unit(wtq, wvq, t0, n, qn, qr):
        bank = PR.next()
        mm(bank, bank[:, 0:n], [(wvq[:, kc, 0:128], mqn[:, kc, t0:t0 + n]) for kc in range(4)], [wtq, mqn])
        bank2 = PR.next()
        mm(bank2, bank2[0:64, 0:n], [(wvq[:, kc, 128:192], mqn[:, kc, t0:t0 + n]) for kc in range(4)], [wtq, mqn])
        yield
        sq = sqp.next()
        act(sq[:, 0:n], bank[:, 0:n], AF.Square, [bank], [sq])
        sqr = sqp.next()
        act(sqr[0:64, 0:n], bank2[0:64, 0:n], AF.Square, [bank2], [sqr])
        sb = PX.next()
        mm(sb, sb[:, 0:n], [(ONES, sq[:, 0:n]), (cm[0:64, 0, :], sqr[0:64, 0:n])], [cm, sq, sqr])
        yield
        r = rstd_from(sb, n, 192.0)
        stt(qn[:, t0:t0 + n], bank[:, 0:n], spt[:, QNN:QNN + 1], r[:, 0:n], ALU.mult, ALU.mult, [bank, spt, r], [qn])
        if P:
            stt(qr[:, t0:t0 + n], bank2[0:64, 0:n], spt[0:64, QNR:QNR + 1], r[0:64, 0:n], ALU.mult, ALU.mult, [bank2, spt, r], [qr])
        else:
            y = sqp.next()
            stt(y[0:64, 0:n], bank2[0:64, 0:n], spt[0:64, QNR:QNR + 1], r[0:64, 0:n], ALU.mult, ALU.mult, [bank2, spt, r], [y])
            yield from rope_tail(y, 64, n, cm[0:64, 3, 0:64], cosd[0:64, t0:t0 + n], sind[0:64, t0:t0 + n], qr[:, t0:t0 + n], qr)

    def mla_v_unit(wtk, wvk, t4, vh):
        bank = PR.next()
        for i in range(4):
            mm(bank, bank[:, i * 128:(i + 1) * 128],
               [(ckvT[:, kc, (t4 + i) * 128:(t4 + i + 1) * 128], wvk[:, kc, 128:256]) for kc in range(4)], [wtk, ckvT])
        yield
        cpy(ACT, vh[:, t4 * 128:(t4 + 4) * 128], bank[:, :], [bank], [vh])

    def mla_prep(head):
        B_ = hb[head % 2]
        kn, krh, vh, qn, qr = B_["kn"], B_["krh"], B_["vh"], B_["qn"], B_["qr"]
        wtk, wvk = wnext(vkvu, 0, 4, 256 * head, 256)
        units = []
        for ti, (t0, n) in enumerate(tl_k):
            def aft(r, t0=t0, n=n, krh=krh):
                tt(DVE, krh[:, t0:t0 + n], krgr[:, t0:t0 + n], r[0:64, 0:n], ALU.mult, [krgr, r], [krh])
            units.append(qk_unit([(wvk[:, kc, 0:128], ckvT[:, kc, t0:t0 + n]) for kc in range(4)], [wtk, ckvT], n,
                                 lambda sq, n=n, t0=t0: ([(ONES, sq[:, 0:n]), (cm[0:64, 0, :], sqkr[0:64, t0:t0 + n])], [sqkr]),
                                 192.0, spt[:, KNN:KNN + 1], kn[:, t0:t0 + n], kn, after=aft))
        for t4 in range(0, NK // 128, 4):
            units.append(mla_v_unit(wtk, wvk, t4, vh))
        pipeline(units)
        wtq, wvq = wnext(vqu, 0, 4, 192 * head, 192)
        pipeline([mla_q_unit(wtq, wvq, t0, n, qn, qr) for (t0, n) in tl_q])

    def mla_attn(head):
        B_ = hb[head % 2]
        kn, krh, vh, qn, qr = B_["kn"], B_["krh"], B_["vh"], B_["qn"], B_["qr"]
        if head == 0:
            ctx["dump"]("kn", kn[:], [128, NK], [kn], bf16=True)
        if P:
            qtl = [(s * 256, 256, [2 * s, 2 * s + 1]) for s in range(2)]
        else:
            qtl = [(t0, n, list(range(NK // 128))) for (t0, n) in tl_q]

        def sc_(kt, q0, n):
            return ([(kn[:, kt * 128:(kt + 1) * 128], qn[:, q0:q0 + n]), (krh[:, kt * 128:(kt + 1) * 128], qr[:, q0:q0 + n])],
                    [kn, krh, qn, qr], 192.0 ** -0.5)

        def vf_(kt):
            return vh[:, kt * 128:(kt + 1) * 128], [vh]

        def fa_(q0, n, Ob, Db):
            r1 = f32p.next()
            S.op(DVE, lambda: nc.vector.reciprocal(out=r1[:, 0:n], in_=Db[:, 0:n]), reads=[Db], writes=[r1])
            tt(DVE, oT[:, 8 + head, q0:q0 + n], Ob[:, 0:n], r1[:, 0:n], ALU.mult, [Ob, r1], [oT])
        for (q0, n, keyts) in qtl:
            attn_unit(q0, n, keyts, sc_, vf_, fa_)
    mla_prep(0)
    for head in range(8):
        if head + 1 < 8:
            mla_prep(head + 1)
        mla_attn(head)
        ada_hook(3)
    attn_flush()
    for B_ in hb:
        for t in B_.values():
            AR.release(t)
    for t in (mqn, ckvT, krgr, sqkr):
        AR.release(t)

    if first:
        while ctx["ada_step"](PR, PX):
            pass
    if not P:
        xres = AR.alloc("xres", [128, 16, TQM], F32, nsub=16, top=True)
        for c4 in range(4):
            dma(SP, xres[:, c4 * 4:c4 * 4 + 4, 0:640], xT[:, c4 * 4:c4 * 4 + 4, 0:640], [], [(xres, range(c4 * 4, c4 * 4 + 4))])
    ctx["dump"]("oT0", oT[:], [128, 16, TQ0], [oT], bf16=True)
    outproj(d["ab_w_out"][0], G1, DVK(0, cj, 2), oT, xres, TQ0)
    AR.release(oT)
    ctx["dump"]("x_attn0", xres[:], [128, 16, TQM], [xres])
    ffn(0, xres, TQ0)
    ctx["dump"]("x_l0", xres[:], [128, 16, TQM], [xres])

    A1, B1, G1 = DV(1, cj, 0), DV(1, cj, 1), DV(1, cj, 2)
    tl_kv = ntiles(TKV1)
    tl_q = ntiles(TQ1)
    h = AR.alloc("hL1", [128, 16, TKV1], BF16, nsub=len(tl_kv))
    rms_mod(lambda ti, t0, n: (xres, lambda c: xres[:, c, t0:t0 + n]), tl_kv, A1, B1, DVK(1, cj, 0), DVK(1, cj, 1), h)
    vin = wview(d["c_w_in"][0])
    NK = KC1 + TKV1
    NKT = NK // 128
    qT = AR.alloc("qTgq", [128, 16, TQ1], BF16)
    kT = AR.alloc("kTgq", [128, 4, NK], BF16)
    V = AR.alloc("Vgq", [128, NKT, 512], BF16)
    cosg, sing = (None, None) if P else (ropet[:, 0, :], ropet[:, 1, :])
    if not P:
        for i in range(2):
            dma(SP, ropet[:, i, :], d["rope"][:, 2 + i, :], [], [ropet])
        for kvh in range(4):
            dma(POOL, kT[:, kvh, 0:512], d["c_gqkT"][kvh], [], [kT])
        for t4 in range(4):
            dma(POOL, V[:, t4, :], d["c_gqv"][t4 * 128:(t4 + 1) * 128, :], [], [V])
    units = []
    wt, wv = wnext(vin, 0, 16, 2048, 512)
    for kvh in range(4):
        for ti, (t0, n) in enumerate(tl_kv):
            rp = None if P else (RGQ, cosg[:, t0:t0 + n], sing[:, t0:t0 + n])
            o32 = d["o_gqk"][kvh, :, tok0 + t0:tok0 + t0 + n] if P else None
            units.append(qk_unit([(wv[:, kc, kvh * 128:(kvh + 1) * 128], h[:, kc, t0:t0 + n]) for kc in range(16)], [wt, (h, ti)],
                                 n, lambda sq, n=n: ([(ONES, sq[:, 0:n])], []), 128.0, spt[:, GQKN:GQKN + 1],
                                 kT[:, kvh, KC1 + t0:KC1 + t0 + n], kT, rope=rp, out32=o32))
    pipeline(units)
    units = []
    wt, wv = wnext(vin, 0, 16, 2560, 512)
    for t4 in range(TKV1 // 128):
        o32 = d["o_gqv"][tok0 + t4 * 128:tok0 + (t4 + 1) * 128, :] if P else None
        units.append(v_unit([(h[:, kc, t4 * 128:(t4 + 1) * 128], wv[:, kc, :]) for kc in range(16)], [wt, (h, (t4 * 128) // 512)],
                            V[:, KC1 // 128 + t4, :], V, out32=o32))
    pipeline(units)
    for jw in range(4):
        units = []
        wt, wv = wnext(vin, 0, 16, jw * 512, 512)
        for hh in range(4):
            head = jw * 4 + hh
            for ti, (t0, n) in enumerate(tl_q):
                rp = None if P else (RGQ, cosg[:, t0:t0 + n], sing[:, t0:t0 + n])
                units.append(qk_unit([(wv[:, kc, hh * 128:(hh + 1) * 128], h[:, kc, t0:t0 + n]) for kc in range(16)], [wt, (h, ti)],
                                     n, lambda sq, n=n: ([(ONES, sq[:, 0:n])], []), 128.0, spt[:, GQQN:GQQN + 1],
                                     qT[:, head, t0:t0 + n], qT, rope=rp))
        pipeline(units)
    AR.release(h)
    oT = AR.alloc("oT1", [128, 16, TQ1], BF16)
    if not P:
        maskb = AR.alloc("maskb", [128, 5, 512], BF16)
        dma(POOL, maskb[:], d["maskd"], [], [maskb])
    for head in range(16):
        kvh = head // 4
        if P:
            qtl = [(s * 256, 256, [2 * s, 2 * s + 1]) for s in range(2)]
        else:
            qtl = [(0, 512, list(range(NKT)))]

        def sc_(kt, q0, n, head=head, kvh=kvh):
            pairs = [(kT[:, kvh, kt * 128:(kt + 1) * 128], qT[:, head, q0:q0 + n])]
            rds = [kT, qT]
            if (not P) and kt >= 4:
                pairs.append((IDENT, maskb[:, kt - 4, q0:q0 + n]))
                rds += [cm, maskb]
            return pairs, rds, 128.0 ** -0.5

        def vf_(kt, kvh=kvh):
            return V[:, kt, kvh * 128:(kvh + 1) * 128], [V]

        def fa_(q0, n, Ob, Db, head=head):
            r1 = f32p.next()
            ts(r1[:, 0:n], Db[:, 0:n], ES[:, head:head + 1], None, ALU.add, ALU.bypass, [Db, K_ES], [r1])
            S.op(DVE, lambda: nc.vector.reciprocal(out=r1[:, 0:n], in_=r1[:, 0:n]), reads=[r1], writes=[r1])
            tt(DVE, oT[:, head, q0:q0 + n], Ob[:, 0:n], r1[:, 0:n], ALU.mult, [Ob, r1], [oT])
        for (q0, n, keyts) in qtl:
            attn_unit(q0, n, keyts, sc_, vf_, fa_)
    attn_flush()
    AR.release(qT)
    AR.release(kT)
    AR.release(V)
    if not P:
        AR.release(maskb)
    outproj(d["c_w_out"][0], G1, DVK(1, cj, 2), oT, xres, TQ1)
    AR.release(oT)
    ffn(1, xres, TQ1)
    for c4 in range(4):
        dma(SP, yT[:, c4 * 4:c4 * 4 + 4, 0:TQ1], xres[:, c4 * 4:c4 * 4 + 4, 0:TQ1], [(xres, range(c4 * 4, c4 * 4 + 4))], [])
    AR.release(xres)
    if not P:
        AR.release(ropet)
    for p_ in (sqp, f32p, ep, lnp, rsp, o32p):
        p_.free()


def _consts():
    cm = np.zeros((128, 5, 128), np.float32)
    cm[:, 0, :] = 1.0
    k = np.arange(128)
    cm[:, 1, :] = (k[:, None] // 64 == k[None, :] // 64)
    cm[:, 2, :] = np.eye(128)
    for m in range(128):
        j = m % 32
        if j < 16:
            cm[m + 16, 3, m] = -1.0
        else:
            cm[m - 16, 3, m] = 1.0
        j = m % 64
        if j < 32:
            cm[m + 32, 4, m] = -1.0
        else:
            cm[m - 32, 4, m] = 1.0
    return cm


def _rope_tables(half):
    t = np.arange(1024)
    pos = t if half == 0 else 1023 - t
    row = (pos // 64).astype(np.float32)
    col = (pos % 64).astype(np.float32)
    out = np.zeros((128, 4, 1024), np.float32)
    for p in range(128):
        i = p % 64
        f = (i % 32) % 16
        inv = np.float32(10000.0) ** (-np.float32(f) / np.float32(16))
        ang = (row if (i // 32) == 0 else col) * np.float32(inv)
        out[p, 0] = np.cos(ang)
        out[p, 1] = np.sin(ang)
        f = (p % 64) % 32
        inv = np.float32(10000.0) ** (-np.float32(f) / np.float32(32))
        ang = (row if (p // 64) == 0 else col) * np.float32(inv)
        out[p, 2] = np.cos(ang)
        out[p, 3] = np.sin(ang)
    return out


def _mask():
    m = np.zeros((128, 5, 512), np.float32)
    q = np.arange(512)[None, :]
    for lt in range(5):
        k = (lt * 128 + np.arange(128))[:, None]
        m[:, lt, :] = np.where(np.abs(q - k) <= 128, 0.0, NEG)
    return m


def _pack_sp(inp, cond0, cond1):
    sp = np.zeros((128, NSP), np.float32)

    def fm(v):
        return np.asarray(v, np.float32).reshape(-1, 128).T
    for l in range(2):
        sp[:, N1G + l * 16:N1G + l * 16 + 16] = fm(inp["norm1_g"][l])
        sp[:, N2G + l * 16:N2G + l * 16 + 16] = fm(inp["norm2_g"][l])
        b = fm(inp["ada_b"][l])
        sp[:, ADAB + l * 192:ADAB + (l + 1) * 192] = np.repeat(b, 2, axis=1)
    sp[:, DAQN] = np.tile(inp["da_q_norm"][0], 2)
    sp[:, DAKN] = np.tile(inp["da_k_norm"][0], 2)
    sp[:, SUBLN] = inp["da_subln"][0]
    sp[:, MQN:MQN + 4] = fm(inp["mla_q_a_norm"][0])
    sp[:, MKVN:MKVN + 4] = fm(inp["mla_kv_a_norm"][0])
    sp[:, QNN] = inp["mla_q_norm"][0][:128]
    sp[:64, QNR] = inp["mla_q_norm"][0][128:]
    sp[:, KNN] = inp["mla_k_norm"][0][:128]
    sp[:64, KNR] = inp["mla_k_norm"][0][128:]
    sp[:, GQQN] = inp["gq_q_norm"][0]
    sp[:, GQKN] = inp["gq_k_norm"][0]
    sp[:, SINK:SINK + 16] = np.broadcast_to(inp["gq_sink"][0][None, :], (128, 16))
    sp[:64, LAMC + 0] = inp["da_lambda_q1"][0]
    sp[:64, LAMC + 1] = inp["da_lambda_k1"][0]
    sp[:64, LAMC + 2] = inp["da_lambda_q2"][0]
    sp[:64, LAMC + 3] = inp["da_lambda_k2"][0]
    c0, c1 = fm(cond0), fm(cond1)
    sp[:, CONDT:CONDT + 32:2] = c0
    sp[:, CONDT + 1:CONDT + 32:2] = c1
    return sp


_WKEYS = ["ada_w", "ff1_w", "ff2_w", "ab_w_in", "ab_w_out", "c_w_in", "c_w_out"]


def core_inputs(inp, c, shared):
    sb, half = c // 2, c % 2
    m = dict(shared)
    xp = np.asarray(inp["x_prompt"][4 * c:4 * c + 4]).reshape(1024, 2048)
    m["xTp"] = np.ascontiguousarray(xp.T)
    xs = np.asarray(inp["x_sample"][sb])
    if half == 1:
        xs = xs[::-1]
    m["xTs"] = np.ascontiguousarray(xs.T)
    m["c_dakT"] = np.ascontiguousarray(np.transpose(inp["cache_da_k"][sb, 0], (1, 2, 0)))
    m["c_dav"] = np.ascontiguousarray(np.asarray(inp["cache_da_v"][sb, 0]).reshape(512, 1024))
    m["c_ckvT"] = np.ascontiguousarray(np.asarray(inp["cache_mla_ckv"][sb, 0]).T)
    m["c_krT"] = np.ascontiguousarray(np.asarray(inp["cache_mla_krope"][sb, 0]).T)
    m["c_gqkT"] = np.ascontiguousarray(np.transpose(inp["cache_gq_k"][sb, 0], (1, 2, 0)))
    m["c_gqv"] = np.ascontiguousarray(np.asarray(inp["cache_gq_v"][sb, 0]).reshape(512, 512))
    m["sp"] = _pack_sp(inp, inp["c_ctx"], inp["c"][sb])
    m["rope"] = shared["_rope"][half]
    del m["_rope"]
    return m


def shared_inputs(inp):
    sh = {k: np.asarray(inp[k], np.float32) for k in _WKEYS}
    sh["w_q_up"] = np.asarray(inp["mla_w_q_up"], np.float32)
    sh["w_kv_up"] = np.asarray(inp["mla_w_kv_up"], np.float32)
    sh["cmat"] = _consts()
    sh["maskl1"] = _mask()
    sh["_rope"] = [_rope_tables(0), _rope_tables(1)]
    return sh


_PROG = {}


def kernel(**inp):
    if "nc" not in _PROG:
        _PROG["nc"] = build_program()[0]
    nc = _PROG["nc"]
    inp = {k: np.asarray(v) for k, v in inp.items()}
    sh = shared_inputs(inp)
    in_maps = [core_inputs(inp, c, sh) for c in range(8)]
    res = run_bass_kernel_spmd(nc, in_maps, core_ids=list(range(8)))
    R = res.results
    y_prompt = np.zeros((32, 256, 2048), np.float32)
    y_sample = np.zeros((4, 1024, 2048), np.float32)
    n_dak = np.zeros((32, 1, 256, 8, 128), np.float32)
    n_dav = np.zeros((32, 1, 256, 8, 128), np.float32)
    n_ckv = np.zeros((32, 1, 256, 512), np.float32)
    n_kr = np.zeros((32, 1, 256, 64), np.float32)
    n_gqk = np.zeros((32, 1, 256, 4, 128), np.float32)
    n_gqv = np.zeros((32, 1, 256, 4, 128), np.float32)
    for c in range(8):
        r = R[c]
        sb, half = c // 2, c % 2
        y_prompt[4 * c:4 * c + 4] = r["yTp"].T.reshape(4, 256, 2048)
        ys = r["yTs"].T
        if half == 0:
            y_sample[sb, 0:512] = ys
        else:
            y_sample[sb, 512:1024] = ys[::-1]
        n_dak[4 * c:4 * c + 4, 0] = np.transpose(r["o_dak"], (2, 0, 1)).reshape(4, 256, 8, 128)
        n_dav[4 * c:4 * c + 4, 0] = r["o_dav"].reshape(4, 256, 8, 128)
        n_ckv[4 * c:4 * c + 4, 0] = r["o_ckv"].T.reshape(4, 256, 512)
        n_kr[4 * c:4 * c + 4, 0] = r["o_kr"].T.reshape(4, 256, 64)
        n_gqk[4 * c:4 * c + 4, 0] = np.transpose(r["o_gqk"], (2, 0, 1)).reshape(4, 256, 4, 128)
        n_gqv[4 * c:4 * c + 4, 0] = r["o_gqv"].reshape(4, 256, 4, 128)
    return (y_prompt, y_sample, n_dak, n_dav, n_ckv, n_kr, n_gqk, n_gqv)
```

```python
import contextlib
import math
import numpy as np
import concourse.bass as bass
import concourse.mybir as mybir
from concourse.bass_utils import run_bass_kernel_spmd

F32 = mybir.dt.float32
BF16 = mybir.dt.bfloat16
U8 = mybir.dt.uint8
AF = mybir.ActivationFunctionType
ALU = mybir.AluOpType

PE, ACT, DVE, POOL, SP = "pe", "act", "dve", "pool", "sp"
EPS = 1e-6
NEG = -30000.0


class Tile:
    def __init__(self, name, h, nsub=1, psum=False, off=0, nbytes=0, ghosts=()):
        self.name, self.h, self.nsub, self.psum = name, h, nsub, psum
        self.off, self.nbytes = off, nbytes
        self.ghosts = list(ghosts)
        self.summary = None
        self.inherit = None

    def __getitem__(self, idx):
        return self.h[idx]


def _keys(items):
    out = []
    for it in items:
        if isinstance(it, tuple):
            t, sub = it
            if sub is None:
                out.extend((t, s) for s in range(t.nsub))
            elif isinstance(sub, (list, range)):
                out.extend((t, s) for s in sub)
            else:
                out.append((t, sub))
        else:
            out.extend((it, s) for s in range(it.nsub))
    return out


class Op:
    __slots__ = ("eng", "fn", "reads", "writes", "dma", "deps", "has_dep", "token")


class Arena:
    def __init__(self, nc, base, size):
        self.nc, self.base, self.size = nc, base, size
        self.free = [(0, size)]
        self.ghosts = []
        self.n = 0
        self.peak = 0

    def alloc(self, name, shape, dt, nsub=1, top=False):
        esz = {F32: 4, BF16: 2, U8: 1}[dt]
        nb = esz
        for s in shape[1:]:
            nb *= s
        nb = (nb + 31) // 32 * 32
        order = range(len(self.free) - 1, -1, -1) if top else range(len(self.free))
        for i in order:
            s, e = self.free[i]
            if e - s >= nb:
                if e - s == nb:
                    off = s
                    self.free.pop(i)
                elif top:
                    off = e - nb
                    self.free[i] = (s, e - nb)
                else:
                    off = s
                    self.free[i] = (s + nb, e)
                break
        else:
            raise RuntimeError(f"arena OOM allocating {name} {shape} ({nb} B); free={self.free}")
        self.n += 1
        uname = f"{name}_{self.n}"
        h = self.nc.alloc_sbuf_tensor_at(uname, list(shape), dt, offset=self.base + off)
        inh = [g for g in self.ghosts if g[0] < off + nb and g[1] > off]
        self.ghosts = [g for g in self.ghosts if not (g[0] >= off and g[1] <= off + nb)]
        t = Tile(uname, h, nsub=nsub, off=off, nbytes=nb, ghosts=[g[2] for g in inh])
        self.peak = max(self.peak, off + nb)
        return t

    def release(self, t):
        s, e = t.off, t.off + t.nbytes
        self.ghosts.append((s, e, t))
        fr = self.free + [(s, e)]
        fr.sort()
        merged = []
        for a, b in fr:
            if merged and merged[-1][1] == a:
                merged[-1] = (merged[-1][0], b)
            else:
                merged.append((a, b))
        self.free = merged


class Sched:
    def __init__(self, nc, st, n_dma_sems=48):
        self.nc, self.st = nc, st
        self.engs = {PE: nc.tensor, ACT: nc.scalar, DVE: nc.vector, POOL: nc.gpsimd, SP: nc.sync}
        self.ops = []
        self.wloads = []
        self.EPOCH = 3000
        self.n_dma_sems = n_dma_sems

    def op(self, eng, fn, reads=(), writes=(), dma=False, defer_pos=None):
        o = Op()
        o.eng, o.fn, o.dma = eng, fn, dma
        o.reads, o.writes = _keys(reads), _keys(writes)
        o.has_dep, o.token, o.deps = False, None, None
        if defer_pos is None:
            self.ops.append(o)
        else:
            self.wloads.append((defer_pos, o))
        return o

    def pos(self):
        return len(self.ops)

    def _tile_inherit(self, t, ops):
        if t.inherit is not None:
            return t.inherit
        eng_max, dmas = {}, set()
        for g in t.ghosts:
            if g.summary is not None:
                ge, gd = g.summary
            else:
                ge, gd = self._tile_inherit(g, ops)
            for e, i in ge.items():
                if eng_max.get(e, -1) < i:
                    eng_max[e] = i
            dmas |= gd
        t.inherit = (eng_max, dmas)
        return t.inherit

    def finalize(self):
        ins = {}
        for p, o in self.wloads:
            ins.setdefault(p, []).append(o)
        merged = []
        for i, o in enumerate(self.ops):
            if i in ins:
                merged.extend(ins[i])
            merged.append(o)
        if len(self.ops) in ins:
            merged.extend(ins[len(self.ops)])
        ops = merged
        last_w, rd_e, rd_d = {}, {}, {}
        for idx, o in enumerate(ops):
            deps = set()
            rkeys = set(o.reads)
            tiles = {}
            for r in o.reads:
                tiles[r[0].name] = r[0]
                w = last_w.get(r)
                if w is not None:
                    deps.add(w)
                if r[0].psum:
                    deps.update(rd_e.get(r, {}).values())
            for r in o.writes:
                tiles[r[0].name] = r[0]
                w = last_w.get(r)
                if w is not None:
                    deps.add(w)
                deps.update(rd_e.get(r, {}).values())
                deps.update(rd_d.get(r, ()))
            for t in tiles.values():
                if t.ghosts:
                    ge, gd = self._tile_inherit(t, ops)
                    deps.update(ge.values())
                    deps.update(gd)
            fd = []
            for d in deps:
                p = ops[d]
                if p.dma:
                    fd.append(d)
                    continue
                if p.eng == o.eng and not o.dma:
                    if o.eng == PE:
                        continue
                    if not any((k in rkeys) for k in p.writes):
                        continue
                fd.append(d)
            o.deps = fd
            for d in fd:
                ops[d].has_dep = True
            for r in o.reads:
                if o.dma:
                    rd_d.setdefault(r, []).append(idx)
                else:
                    rd_e.setdefault(r, {})[o.eng] = idx
            for r in o.writes:
                last_w[r] = idx
                rd_e[r] = {}
                rd_d[r] = []
            for t in tiles.values():
                if t.summary is None:
                    t.summary = ({}, set())
                if o.dma:
                    t.summary[1].add(idx)
                else:
                    t.summary[0][o.eng] = idx
        self._emit(ops)

    def _emit(self, ops):
        nc, st = self.nc, self.st
        esem = {e: [st.enter_context(nc.semaphore(f"s_{e}0"))] for e in (PE, ACT, DVE, POOL)}
        dsem = [st.enter_context(nc.semaphore(f"d{i}")) for i in range(self.n_dma_sems)]
        cnt = {e: 0 for e in esem}
        ep = {e: 0 for e in esem}
        waited = {e: {} for e in self.engs}
        nd = len(dsem)
        dma_i = 0
        dsem_last = [None] * nd
        dsem_cnt = [0] * nd
        nwaits = 0
        for o in ops:
            e = self.engs[o.eng]
            need = {}
            for d in o.deps:
                key, sem, val = ops[d].token
                if need.get(key, (None, 0))[1] < val:
                    need[key] = (sem, val)
            if o.dma:
                k = dma_i % nd
                dma_i += 1
                if dsem_last[k] is not None:
                    key, sem, val = dsem_last[k]
                    if need.get(key, (None, 0))[1] < val:
                        need[key] = (sem, val)
            for key, (sem, val) in need.items():
                if waited[o.eng].get(key, 0) >= val:
                    continue
                e.wait_ge(sem, val)
                nwaits += 1
                waited[o.eng][key] = val
            ins = o.fn()
            if o.dma:
                dsem_cnt[k] += 16
                ins.then_inc(dsem[k], 16)
                o.token = (("d", k), dsem[k], dsem_cnt[k])
                dsem_last[k] = o.token
            elif o.has_dep:
                if cnt[o.eng] >= self.EPOCH:
                    cnt[o.eng] = 0
                    ep[o.eng] += 1
                    esem[o.eng].append(st.enter_context(nc.semaphore(f"s_{o.eng}{ep[o.eng]}")))
                cnt[o.eng] += 1
                sem = esem[o.eng][-1]
                ins.then_inc(sem, 1)
                o.token = ((o.eng, ep[o.eng]), sem, cnt[o.eng])
        sp = self.engs[SP]
        for k in range(nd):
            if dsem_last[k] is not None:
                key, sem, val = dsem_last[k]
                sp.wait_ge(sem, val)
        self.stats = dict(n_ops=len(ops), n_waits=nwaits, epochs=dict(ep))


N1G, N2G, ADAB = 0, 32, 64
DAQN, DAKN, SUBLN = 448, 449, 450
MQN, MKVN = 451, 455
QNN, QNR, KNN, KNR = 459, 460, 461, 462
GQQN, GQKN = 463, 464
SINK = 465
LAMC = 481
CONDT = 485
NSP = 520
LAM_INIT0 = 0.8 - 0.6 * math.exp(-0.3 * 0)
NSLOT = 4


def build_program(n_pgroups=2, do_sample=True, dbg=None):
    nc = bass.Bass("TRN2", target_bir_lowering=False)
    st = contextlib.ExitStack()
    S = Sched(nc, st)

    def din(name, shape):
        return nc.dram_tensor(name, list(shape), F32, kind="ExternalInput").ap()

    def dout(name, shape):
        return nc.dram_tensor(name, list(shape), F32, kind="ExternalOutput").ap()

    xTp = din("xTp", [2048, 1024])
    xTs = din("xTs", [2048, 1024])
    c_dakT = din("c_dakT", [8, 128, 512])
    c_dav = din("c_dav", [512, 1024])
    c_ckvT = din("c_ckvT", [512, 512])
    c_krT = din("c_krT", [64, 512])
    c_gqkT = din("c_gqkT", [4, 128, 512])
    c_gqv = din("c_gqv", [512, 512])
    spd = din("sp", [128, NSP])
    cmat = din("cmat", [128, 5, 128])
    rope = din("rope", [128, 4, 1024])
    maskd = din("maskl1", [128, 5, 512])
    ada_w = din("ada_w", [2, 2048, 12288])
    ff1_w = din("ff1_w", [2, 2048, 8192])
    ff2_w = din("ff2_w", [2, 8192, 2048])
    ab_w_in = din("ab_w_in", [1, 2048, 4160])
    ab_w_out = din("ab_w_out", [1, 2048, 2048])
    w_q_up = din("w_q_up", [1, 512, 1536])
    w_kv_up = din("w_kv_up", [1, 512, 2048])
    c_w_in = din("c_w_in", [1, 2048, 3072])
    c_w_out = din("c_w_out", [1, 2048, 2048])

    yTp = dout("yTp", [2048, 1024])
    yTs = dout("yTs", [2048, 512])
    o_dak = dout("o_dak", [8, 128, 1024])
    o_dav = dout("o_dav", [1024, 1024])
    o_ckv = dout("o_ckv", [512, 1024])
    o_kr = dout("o_kr", [64, 1024])
    o_gqk = dout("o_gqk", [4, 128, 1024])
    o_gqv = dout("o_gqv", [1024, 512])

    def wview(w):
        return w.rearrange("(kc p) n -> p kc n", p=128)

    ARENA_BYTES = 207 * 1024
    slab = st.enter_context(nc.sbuf_tensor("arena", [128, ARENA_BYTES], U8))
    base = nc.lookup_mloc(slab).addr
    AR = Arena(nc, base, ARENA_BYTES)

    PSB = []
    for b in range(8):
        h = st.enter_context(nc.psum_tensor(f"ps{b}", [128, 512], F32))
        PSB.append(Tile(f"ps{b}", h, psum=True))

    class Rot:
        def __init__(self, banks):
            self.banks, self.i = banks, 0

        def next(self):
            b = self.banks[self.i % len(self.banks)]
            self.i += 1
            return b

    slots = []
    for i in range(NSLOT):
        t = AR.alloc(f"wslot{i}", [128, 16, 512], BF16)
        t.views = {(16, 512): t.h}
        slots.append(t)

    def slot_view(t, nkc, ncols):
        key = (nkc, ncols)
        if key not in t.views:
            AR.n += 1
            t.views[key] = nc.alloc_sbuf_tensor_at(f"{t.name}_v{AR.n}", [128, nkc, ncols], BF16,
                                                   offset=AR.base + t.off)
        return t.views[key]

    wstate = dict(j=0, reqpos=[])

    def wnext(view, kc0, nkc, c0, ncols):
        j = wstate["j"]
        wstate["j"] += 1
        t = slots[j % NSLOT]
        assert not wstate["reqpos"] or S.pos() > wstate["reqpos"][-1], "ring tiles must be requested in use order"
        wstate["reqpos"].append(S.pos())
        pos = wstate["reqpos"][max(0, j - NSLOT + 1)]
        dst = slot_view(t, nkc, ncols)
        src = view[:, kc0:kc0 + nkc, c0:c0 + ncols]
        S.op(POOL, lambda: nc.gpsimd.dma_start(out=dst[:], in_=src), writes=[t], dma=True, defer_pos=pos)
        return t, dst

    def mm(bank, out_ap, pairs, reads, start=True, stop=True):
        pairs = list(pairs)

        def fn():
            ins = None
            n = len(pairs)
            for i, (l, r) in enumerate(pairs):
                ins = nc.tensor.matmul(out_ap, lhsT=l, rhs=r, start=(start and i == 0), stop=(stop and i == n - 1))
            return ins
        S.op(PE, fn, reads=reads, writes=[bank])

    def act(out, in_, func, reads, writes, scale=1.0, bias=None):
        if bias is None:
            S.op(ACT, lambda: nc.scalar.activation(out=out, in_=in_, func=func, scale=scale), reads=reads, writes=writes)
        else:
            S.op(ACT, lambda: nc.scalar.activation(out=out, in_=in_, func=func, scale=scale, bias=bias), reads=reads, writes=writes)

    def tt(eng, out, a, b, op, reads, writes):
        e = nc.vector if eng == DVE else nc.gpsimd
        S.op(eng, lambda: e.tensor_tensor(out=out, in0=a, in1=b, op=op), reads=reads, writes=writes)

    def ts(out, a, s1, s2, op0, op1, reads, writes):
        if s2 is None:
            S.op(DVE, lambda: nc.vector.tensor_scalar(out=out, in0=a, scalar1=s1, scalar2=None, op0=op0), reads=reads, writes=writes)
        else:
            S.op(DVE, lambda: nc.vector.tensor_scalar(out=out, in0=a, scalar1=s1, scalar2=s2, op0=op0, op1=op1), reads=reads, writes=writes)

    def stt(out, in0, scalar, in1, op0, op1, reads, writes):
        S.op(DVE, lambda: nc.vector.scalar_tensor_tensor(out=out, in0=in0, scalar=scalar, in1=in1, op0=op0, op1=op1), reads=reads, writes=writes)

    def cpy(eng, out, in_, reads, writes):
        if eng == ACT:
            S.op(ACT, lambda: nc.scalar.activation(out=out, in_=in_, func=AF.Copy), reads=reads, writes=writes)
        elif eng == DVE:
            S.op(DVE, lambda: nc.vector.tensor_copy(out=out, in_=in_), reads=reads, writes=writes)
        else:
            S.op(POOL, lambda: nc.gpsimd.tensor_copy(out=out, in_=in_), reads=reads, writes=writes)

    def dma(eng, out, in_, reads, writes):
        e = nc.sync if eng == SP else nc.gpsimd
        S.op(eng, lambda: e.dma_start(out=out, in_=in_), reads=reads, writes=writes, dma=True)

    spt = AR.alloc("sp", [128, NSP], F32)
    dma(SP, spt[:], spd, [], [spt])
    cm = AR.alloc("cm", [128, 5, 128], BF16)
    dma(POOL, cm[:], cmat, [], [cm])
    cm32 = AR.alloc("cm32", [128, 2, 128], F32)
    dma(SP, cm32[:, 0, :], cmat[:, 0, :], [], [cm32])
    dma(SP, cm32[:, 1, :], cmat[:, 2, :], [], [cm32])
    ONES, BONES, IDENT, RDA, RGQ = (cm[:, i, :] for i in range(5))
    epsc = AR.alloc("epsc", [128, 1], F32)
    S.op(DVE, lambda: nc.vector.memset(epsc[:], EPS), writes=[epsc])

    dv = AR.alloc("dv", [128, 2 * 2 * 6 * 16 + 64], F32)

    def DV(l, j, kind):
        o = ((l * 2 + j) * 6 + kind) * 16
        return dv.h[:, o:o + 16]
    DVX = 2 * 2 * 6 * 16
    ES = dv.h[:, DVX:DVX + 16]
    NEGLAM = dv.h[:, DVX + 16:DVX + 17]
    SUBLN_EFF = dv.h[:, DVX + 17:DVX + 18]

    PR = Rot(PSB[0:5])
    PX = Rot(PSB[5:8])

    silu = AR.alloc("silu", [128, 32], BF16)
    act(silu[:], spt[:, CONDT:CONDT + 32], AF.Silu, [spt], [silu])
    mod = [AR.alloc(f"mod{l}", [128, 192], F32, nsub=24) for l in range(2)]
    mtoks = [AR.alloc(f"mtok{i}", [2, 512], F32) for i in range(2)]
    dv.nsub = 27

    def DVK(l, j, kind):
        return (dv, (l * 2 + j) * 6 + kind)
    K_ES, K_NEGLAM, K_SUBLN = (dv, 24), (dv, 25), (dv, 26)
    ada_state = dict(next=0)

    def derive(l, what):
        m3 = mod[l].h[:].rearrange("p (c j) -> p c j", j=2)
        n1 = spt[:, N1G + l * 16:N1G + l * 16 + 16]
        n2 = spt[:, N2G + l * 16:N2G + l * 16 + 16]
        for j in range(2):
            if what == 0:
                stt(DV(l, j, 0), m3[:, 16:32, j], 1.0, n1, ALU.add, ALU.mult, [(mod[l], range(4, 8)), spt], [DVK(l, j, 0)])
                cpy(DVE, DV(l, j, 1), m3[:, 0:16, j], [(mod[l], range(0, 4))], [DVK(l, j, 1)])
            elif what == 1:
                cpy(DVE, DV(l, j, 2), m3[:, 32:48, j], [(mod[l], range(8, 12))], [DVK(l, j, 2)])
            elif what == 2:
                stt(DV(l, j, 3), m3[:, 64:80, j], 1.0, n2, ALU.add, ALU.mult, [(mod[l], range(16, 20)), spt], [DVK(l, j, 3)])
                cpy(DVE, DV(l, j, 4), m3[:, 48:64, j], [(mod[l], range(12, 16))], [DVK(l, j, 4)])
            else:
                cpy(DVE, DV(l, j, 5), m3[:, 80:96, j], [(mod[l], range(20, 24))], [DVK(l, j, 5)])

    def ada_step(rot_a, rot_b):
        k = ada_state["next"]
        if k >= 48:
            return False
        ada_state["next"] += 1
        l, j4 = divmod(k, 24)
        vw = wview(ada_w[l])
        wt, wv = wnext(vw, 0, 16, j4 * 512, 512)
        bank = rot_a.next()
        mm(bank, bank[0:2, :], [(silu[:, kc * 2:kc * 2 + 2], wv[:, kc, :]) for kc in range(16)], [silu, wt])
        mt = mtoks[k % 2]
        cpy(ACT, mt[:], bank[0:2, :], [bank], [mt])
        bt = rot_b.next()

        def fn(bt=bt, mt=mt):
            ins = None
            for i in range(4):
                ins = nc.tensor.transpose(bt[:, 2 * i:2 * i + 2], mt[0:2, 128 * i:128 * i + 128], cm32[0:2, 1, 0:2])
            return ins
        S.op(PE, fn, reads=[mt, cm32], writes=[bt])
        c0 = ADAB + l * 192 + j4 * 8
        tt(DVE, mod[l][:, j4 * 8:j4 * 8 + 8], bt[:, 0:8], spt[:, c0:c0 + 8], ALU.add, [bt, spt], [(mod[l], j4)])
        if j4 in (7, 11, 19, 23):
            derive(l, {7: 0, 11: 1, 19: 2, 23: 3}[j4])
        return True

    for _ in range(8):
        ada_step(PR, PX)
    act(ES, spt[:, SINK:SINK + 16], AF.Exp, [spt], [K_ES])
    lamp = AR.alloc("lamp", [128, 2], F32)
    S.op(DVE, lambda: nc.vector.memset(lamp[:], 0.0), writes=[lamp])
    tt(DVE, lamp[0:64, 0:1], spt[0:64, LAMC:LAMC + 1], spt[0:64, LAMC + 1:LAMC + 2], ALU.mult, [spt, lamp], [lamp])
    tt(DVE, lamp[0:64, 1:2], spt[0:64, LAMC + 2:LAMC + 3], spt[0:64, LAMC + 3:LAMC + 4], ALU.mult, [spt, lamp], [lamp])
    bl = PX.next()
    mm(bl, bl[:, 0:2], [(cm32[0:64, 0, :], lamp[0:64, :])], [cm32, lamp])
    lame = AR.alloc("lame", [128, 2], F32)
    act(lame[:], bl[:, 0:2], AF.Exp, [bl], [lame])
    stt(NEGLAM, lame[:, 1:2], -LAM_INIT0, lame[:, 0:1], ALU.add, ALU.subtract, [lame], [K_NEGLAM])
    ts(SUBLN_EFF, spt[:, SUBLN:SUBLN + 1], 1.0 - LAM_INIT0, None, ALU.mult, ALU.bypass, [spt], [K_SUBLN])

    dbg_outs = {}

    def dump(name, ap, shape, reads, bf16=False):
        if dbg is None or name not in dbg or name in dbg_outs:
            return
        o = dout("dbg_" + name, shape)
        dbg_outs[name] = o
        dma(POOL if bf16 else SP, o, ap, reads, [])
    ctx = dict(dbg_on=dbg is not None, dump=dump, nc=nc, S=S, AR=AR, PSB=PSB, PR=PR, PX=PX, mm=mm, act=act, tt=tt, ts=ts,
               stt=stt, cpy=cpy, dma=dma, wnext=wnext, wview=wview, spt=spt, ONES=ONES, BONES=BONES, IDENT=IDENT,
               RDA=RDA, RGQ=RGQ, cm=cm, epsc=epsc, DV=DV, DVK=DVK, dv=dv, ES=ES, NEGLAM=NEGLAM, SUBLN_EFF=SUBLN_EFF,
               K_ES=K_ES, K_NEGLAM=K_NEGLAM, K_SUBLN=K_SUBLN, Rot=Rot, ada_step=ada_step)
    d = dict(xTp=xTp, xTs=xTs, c_dakT=c_dakT, c_dav=c_dav, c_ckvT=c_ckvT, c_krT=c_krT, c_gqkT=c_gqkT, c_gqv=c_gqv,
             rope=rope, maskd=maskd, ff1_w=ff1_w, ff2_w=ff2_w, ab_w_in=ab_w_in, ab_w_out=ab_w_out, w_q_up=w_q_up,
             w_kv_up=w_kv_up, c_w_in=c_w_in, c_w_out=c_w_out, yTp=yTp, yTs=yTs, o_dak=o_dak, o_dav=o_dav,
             o_ckv=o_ckv, o_kr=o_kr, o_gqk=o_gqk, o_gqv=o_gqv)
    first = True

    ada_fin = dict(done=False)

    def ada_done():
        if ada_fin["done"]:
            return
        ada_fin["done"] = True
        while ada_step(PR, PX):
            pass
        for t in [silu, lamp, lame] + mod + mtoks:
            AR.release(t)
    ctx["ada_done"] = ada_done
    if do_sample:
        run_group(ctx, d, kind="S", g=0, first=first)
        ada_done()
        first = False
    for g in range(n_pgroups):
        run_group(ctx, d, kind="P", g=g, first=first)
        if first:
            ada_done()
        first = False
    S.finalize()
    st.close()
    return nc, S, AR


def ntiles(T):
    out, t = [], 0
    while t < T:
        n = min(512, T - t)
        out.append((t, n))
        t += n
    return out


def pipeline(units, limits=None):
    limits = limits or {}
    active = []
    cnt = {}
    units = list(units)
    i = 0

    def adv(entry):
        try:
            next(entry[1])
            return True
        except StopIteration:
            active.remove(entry)
            cnt[entry[0]] -= 1
            return False
    while i < len(units) or active:
        old = list(active)
        if i < len(units):
            tag, fac = units[i]
            if cnt.get(tag, 0) < limits.get(tag, 4):
                i += 1
                e = (tag, fac())
                active.append(e)
                cnt[tag] = cnt.get(tag, 0) + 1
                adv(e)
        for e in old:
            adv(e)


def run_group(ctx, d, kind, g, first=False):
    nc, S, AR, PSB, PR, PX = ctx["nc"], ctx["S"], ctx["AR"], ctx["PSB"], ctx["PR"], ctx["PX"]
    mm, act, tt, ts, stt, cpy, dma = ctx["mm"], ctx["act"], ctx["tt"], ctx["ts"], ctx["stt"], ctx["cpy"], ctx["dma"]
    wnext, wview, spt, DV, DVK, dv = ctx["wnext"], ctx["wview"], ctx["spt"], ctx["DV"], ctx["DVK"], ctx["dv"]
    ONES, BONES, IDENT, RDA, RGQ, cm, epsc = ctx["ONES"], ctx["BONES"], ctx["IDENT"], ctx["RDA"], ctx["RGQ"], ctx["cm"], ctx["epsc"]
    ES, NEGLAM, SUBLN_EFF, Rot = ctx["ES"], ctx["NEGLAM"], ctx["SUBLN_EFF"], ctx["Rot"]
    K_ES, K_NEGLAM, K_SUBLN = ctx["K_ES"], ctx["K_NEGLAM"], ctx["K_SUBLN"]
    P = kind == "P"
    cj = 0 if P else 1
    if P:
        TQ0, TKV0, TQ1, TKV1, KC0, KC1 = 512, 512, 512, 512, 0, 0
        xT = d["xTp"].rearrange("(c p) t -> p c t", p=128)[:, :, g * 512:(g + 1) * 512]
        yT = d["yTp"].rearrange("(c p) t -> p c t", p=128)[:, :, g * 512:(g + 1) * 512]
        tok0 = g * 512
    else:
        TQ0, TKV0, TQ1, TKV1, KC0, KC1 = 640, 1024, 512, 640, 512, 512
        xT = d["xTs"].rearrange("(c p) t -> p c t", p=128)
        yT = d["yTs"].rearrange("(c p) t -> p c t", p=128)
        tok0 = 0
    TQM = max(TQ0, TQ1)
    SB4 = Rot(PSB[0:4])

    def ada_unit(n):
        for _ in range(n):
            ctx["ada_step"](SB4, SB4)
            yield

    class Pool_:
        def __init__(self, name, shape, dt, n):
            self.t = [AR.alloc(f"{name}{i}", shape, dt) for i in range(n)]
            self.i = 0

        def next(self):
            t = self.t[self.i % len(self.t)]
            self.i += 1
            return t

        def free(self):
            for t in self.t:
                AR.release(t)

    sqp = Pool_("sq", [128, 512], BF16, 5)
    f32p = Pool_("f32s", [128, 512], F32, 4)
    ep = Pool_("E", [128, 512], BF16, 5)
    lnp = Pool_("lnp", [128, 512], F32, 2)
    rsp = Pool_("rsp", [128, 512], F32, 2)
    o32p = Pool_("o32", [128, 512], F32, 3 if P else 0)

    def rstd_from(bank, n, D, np_=128):
        l = lnp.next()
        act(l[0:np_, 0:n], bank[0:np_, 0:n], AF.Ln, [bank, epsc], [l], scale=1.0 / D, bias=epsc[0:np_, :])
        r = rsp.next()
        act(r[0:np_, 0:n], l[0:np_, 0:n], AF.Exp, [l], [r], scale=-0.5)
        return r

    rstate = dict(i=0)

    def recip(out_t, out_ap, in_t, in_ap, n):
        rstate["i"] += 1
        if rstate["i"] % 2 == 0:
            S.op(DVE, lambda: nc.vector.reciprocal(out=out_ap, in_=in_ap), reads=[in_t], writes=[out_t])
        else:
            l = lnp.next()
            act(l[:, 0:n], in_ap, AF.Ln, [in_t], [l])
            act(out_ap, l[:, 0:n], AF.Exp, [l], [out_t], scale=-1.0)

    def rms_mod(xsrc, tiles, A, B, KA, KB, h):
        for ti, (t0, n) in enumerate(tiles):
            xt, xap = xsrc(ti, t0, n)
            sb = PX.next()
            for c in range(16):
                sq = sqp.next()
                act(sq[:, 0:n], xap(c), AF.Square, [xt], [sq])
                mm(sb, sb[:, 0:n], [(ONES, sq[:, 0:n])], [cm, sq], start=(c == 0), stop=(c == 15))
            r = rstd_from(sb, n, 2048.0)
            for c in range(16):
                tmp = f32p.next()
                stt(tmp[:, 0:n], xap(c), A[:, c:c + 1], r[:, 0:n], ALU.mult, ALU.mult, [xt, KA, r], [tmp])
                act(h[:, c, t0:t0 + n], tmp[:, 0:n], AF.Identity, [tmp, KB], [(h, ti)], scale=1.0, bias=B[:, c:c + 1])

    ropet = None
    if not P:
        ropet = AR.alloc("ropet", [128, 2, 1024], F32)
        for i in range(2):
            dma(SP, ropet[:, i, :], d["rope"][:, i, :], [], [ropet])

    def rope_tail(y, np_, n, Rm, cosap, sinap, out_ap, out_t):
        rb = PX.next()
        mm(rb, rb[0:np_, 0:n], [(Rm, y[0:np_, 0:n])], [cm, y])
        yield
        t1 = f32p.next()
        tt(DVE, t1[0:np_, 0:n], y[0:np_, 0:n], cosap, ALU.mult, [y, ropet], [t1])
        t2 = f32p.next()
        tt(DVE, t2[0:np_, 0:n], rb[0:np_, 0:n], sinap, ALU.mult, [rb, ropet], [t2])
        tt(DVE, out_ap, t1[0:np_, 0:n], t2[0:np_, 0:n], ALU.add, [t1, t2], [out_t])

    def qk_unit(proj_pairs, proj_reads, n, ss_fn, D, gain, out_ap, out_t, rope=None, out32=None, after=None):
        bank = PR.next()
        mm(bank, bank[:, 0:n], proj_pairs, proj_reads)
        yield
        sq = sqp.next()
        act(sq[:, 0:n], bank[:, 0:n], AF.Square, [bank], [sq])
        sb = PX.next()
        pairs, rds = ss_fn(sq)
        mm(sb, sb[:, 0:n], pairs, [cm, sq] + list(rds))
        yield
        r = rstd_from(sb, n, D)
        if after is not None:
            after(r)
        if rope is None:
            if out32 is None:
                stt(out_ap, bank[:, 0:n], gain, r[:, 0:n], ALU.mult, ALU.mult, [bank, spt, r], [out_t])
            else:
                o32t = o32p.next()
                stt(o32t[:, 0:n], bank[:, 0:n], gain, r[:, 0:n], ALU.mult, ALU.mult, [bank, spt, r], [o32t])
                cpy(ACT, out_ap, o32t[:, 0:n], [o32t], [out_t])
                dma(SP, out32, o32t[:, 0:n], [o32t], [])
        else:
            Rm, cosap, sinap = rope
            y = sqp.next()
            stt(y[:, 0:n], bank[:, 0:n], gain, r[:, 0:n], ALU.mult, ALU.mult, [bank, spt, r], [y])
            yield from rope_tail(y, 128, n, Rm, cosap, sinap, out_ap, out_t)

    def v_unit(pairs, reads, out_ap, out_t, out32=None):
        bank = PR.next()
        mm(bank, bank[:, :], pairs, reads)
        yield
        cpy(ACT, out_ap, bank[:, :], [bank], [out_t])
        if out32 is not None:
            o32t = o32p.next()
            cpy(DVE, o32t[:, :], bank[:, :], [bank], [o32t])
            dma(SP, out32, o32t[:, :], [o32t], [])

    ACC = [(PSB[4], PSB[5]), (PSB[6], PSB[7])]
    astate = dict(u=0, pending=[])

    def attn_unit(q0, n, keyts, score_fn, v_fn, fin_a, fin_b=None, lag=1):
        nk = len(keyts)
        bsz = max(1, 512 // n)
        batches = [keyts[i:i + bsz] for i in range(0, nk, bsz)]
        nb = len(batches)
        acc = None
        es = {}
        for step in range(nb + lag):
            if step < nb:
                sbk = SB4.next()
                scale = None
                for bi, kt in enumerate(batches[step]):
                    pairs, rds, scale = score_fn(kt, q0, n)
                    mm(sbk, sbk[:, bi * n:(bi + 1) * n], pairs, rds)
                e = ep.next()
                w = len(batches[step]) * n
                act(e[:, 0:w], sbk[:, 0:w], AF.Exp, [sbk], [e], scale=scale)
                es[step] = e
            j = step - lag
            if j >= 0:
                if acc is None:
                    acc = ACC[astate["u"] % 2]
                    astate["u"] += 1
                Ob, Db = acc
                e = es.pop(j)
                for bi, kt in enumerate(batches[j]):
                    first_ = (j == 0 and bi == 0)
                    last_ = (j == nb - 1 and bi == len(batches[j]) - 1)
                    vap, vr = v_fn(kt)
                    mm(Ob, Ob[:, 0:n], [(vap, e[:, bi * n:(bi + 1) * n])], [e] + list(vr), start=first_, stop=last_)
                    mm(Db, Db[:, 0:n], [(ONES, e[:, bi * n:(bi + 1) * n])], [e, cm], start=first_, stop=last_)
            if step < nb + lag - 1:
                yield
        Ob, Db = acc
        fin_a(q0, n, Ob, Db)
        if fin_b is not None:
            yield
            yield
            fin_b()

    def ffn(l, xres, TQ):
        A2, B2, G2 = DV(l, cj, 3), DV(l, cj, 4), DV(l, cj, 5)
        tl = ntiles(TQ)
        h2 = AR.alloc("h2", [128, 16, TQ], BF16, nsub=len(tl), top=True)
        rms_mod(lambda ti, t0, n: (xres, lambda c: xres[:, c, t0:t0 + n]), tl, A2, B2, DVK(l, cj, 3), DVK(l, cj, 4), h2)
        v1 = wview(d["ff1_w"][l])
        v2 = wview(d["ff2_w"][l])
        nsplit = 4 if TQ > 512 else 2
        nw1 = 16 // nsplit
        kcs = 64 // nsplit
        ncol2 = 8192 // kcs
        for half in range(nsplit):
            h1 = AR.alloc("h1", [128, kcs, TQ], BF16, nsub=len(tl))
            for jw in range(nw1):
                wt, wv = wnext(v1, 0, 16, (half * nw1 + jw) * 512, 512)
                for fc in range(4):
                    for ti, (t0, n) in enumerate(tl):
                        bank = PR.next()
                        mm(bank, bank[:, 0:n], [(wv[:, kc, fc * 128:(fc + 1) * 128], h2[:, kc, t0:t0 + n]) for kc in range(16)],
                           [wt, (h2, ti)])
                        r = sqp.next()
                        act(r[:, 0:n], bank[:, 0:n], AF.Relu, [bank], [r])
                        tt(DVE, h1[:, jw * 4 + fc, t0:t0 + n], r[:, 0:n], r[:, 0:n], ALU.mult, [r], [(h1, ti)])
            for jd in range(2048 // ncol2):
                wt, wv = wnext(v2, half * kcs, kcs, jd * ncol2, ncol2)
                for dc in range(ncol2 // 128):
                    ch = jd * (ncol2 // 128) + dc
                    for ti, (t0, n) in enumerate(tl):
                        bank = PR.next()
                        mm(bank, bank[:, 0:n], [(wv[:, kc, dc * 128:(dc + 1) * 128], h1[:, kc, t0:t0 + n]) for kc in range(kcs)],
                           [wt, (h1, ti)])
                        stt(xres[:, ch, t0:t0 + n], bank[:, 0:n], G2[:, ch:ch + 1], xres[:, ch, t0:t0 + n], ALU.mult, ALU.add,
                            [bank, DVK(l, cj, 5), (xres, ch)], [(xres, ch)])
            AR.release(h1)
        AR.release(h2)

    def outproj(wd, G, KG, oT, xres, TQ):
        vo = wview(wd)
        tl = ntiles(TQ)
        for jw in range(4):
            wt, wv = wnext(vo, 0, 16, jw * 512, 512)
            for dc in range(4):
                ch = jw * 4 + dc
                for ti, (t0, n) in enumerate(tl):
                    bank = PR.next()
                    mm(bank, bank[:, 0:n], [(wv[:, c, dc * 128:(dc + 1) * 128], oT[:, c, t0:t0 + n]) for c in range(16)], [wt, oT])
                    stt(xres[:, ch, t0:t0 + n], bank[:, 0:n], G[:, ch:ch + 1], xres[:, ch, t0:t0 + n], ALU.mult, ALU.add,
                        [bank, KG, (xres, ch)], [(xres, ch)])

    A1, B1, G1 = DV(0, cj, 0), DV(0, cj, 1), DV(0, cj, 2)
    xres = AR.alloc("xres", [128, 16, TQM], F32, nsub=16, top=True) if P else None
    tl_kv = ntiles(TKV0)
    tl_q = ntiles(TQ0)
    h = AR.alloc("h", [128, 16, TKV0], BF16, nsub=len(tl_kv))
    if P:
        for c4 in range(4):
            dma(SP, xres[:, c4 * 4:c4 * 4 + 4, 0:512], xT[:, c4 * 4:c4 * 4 + 4, :], [], [(xres, range(c4 * 4, c4 * 4 + 4))])
        rms_mod(lambda ti, t0, n: (xres, lambda c: xres[:, c, t0:t0 + n]), tl_kv, A1, B1, DVK(0, cj, 0), DVK(0, cj, 1), h)
    else:
        xst = AR.alloc("xst", [128, 16, 512], F32)

        def xsrc(ti, t0, n):
            for c4 in range(4):
                dma(SP, xst[:, c4 * 4:c4 * 4 + 4, 0:n], xT[:, c4 * 4:c4 * 4 + 4, t0:t0 + n], [], [xst])
            return xst, (lambda c: xst[:, c, 0:n])
        rms_mod(xsrc, tl_kv, A1, B1, DVK(0, cj, 0), DVK(0, cj, 1), h)
        AR.release(xst)
    ctx["dump"]("h", h[:], [128, 16, TKV0], [h], bf16=True)
    oT = AR.alloc("oT", [128, 16, TQ0], BF16)
    vin = wview(d["ab_w_in"][0])
    vqu = wview(d["w_q_up"][0])
    vkvu = wview(d["w_kv_up"][0])

    NKT = (KC0 + TKV0) // 128
    cosd, sind = (None, None) if P else (ropet[:, 0, :], ropet[:, 1, :])
    dat = [AR.alloc(f"dat{i}", [128, 512], F32) for i in range(4)]
    for hp in range(2):
        qT = AR.alloc("qTda", [128, 4, TQ0], BF16)
        kT = AR.alloc("kTda", [128, 4, KC0 + TKV0], BF16)
        V = AR.alloc("Vda", [128, NKT, 512], BF16)
        if not P:
            for hh in range(4):
                dma(POOL, kT[:, hh, 0:512], d["c_dakT"][hp * 4 + hh], [], [kT])
            for t4 in range(4):
                dma(POOL, V[:, t4, :], d["c_dav"][t4 * 128:(t4 + 1) * 128, hp * 512:(hp + 1) * 512], [], [V])
        units = []
        wh = {}

        def wtile(key, *a):
            if key not in wh:
                wh[key] = wnext(*a)
            return wh[key]

        def f_q(hh, ti, t0, n, hp=hp, qT=qT):
            wt, wv = wtile("q", vin, 0, 16, hp * 512, 512)
            rp = None if P else (RDA, cosd[:, t0:t0 + n], sind[:, t0:t0 + n])
            return qk_unit([(wv[:, kc, hh * 128:(hh + 1) * 128], h[:, kc, t0:t0 + n]) for kc in range(16)], [wt, (h, ti)],
                           n, lambda sq: ([(BONES, sq[:, 0:n])], []), 64.0, spt[:, DAQN:DAQN + 1],
                           qT[:, hh, t0:t0 + n], qT, rope=rp)

        def f_k(hh, ti, t0, n, hp=hp, kT=kT):
            wt, wv = wtile("k", vin, 0, 16, 1024 + hp * 512, 512)
            rp = None if P else (RDA, cosd[:, t0:t0 + n], sind[:, t0:t0 + n])
            o32 = d["o_dak"][hp * 4 + hh, :, tok0 + t0:tok0 + t0 + n] if P else None
            return qk_unit([(wv[:, kc, hh * 128:(hh + 1) * 128], h[:, kc, t0:t0 + n]) for kc in range(16)], [wt, (h, ti)],
                           n, lambda sq: ([(BONES, sq[:, 0:n])], []), 64.0, spt[:, DAKN:DAKN + 1],
                           kT[:, hh, KC0 + t0:KC0 + t0 + n], kT, rope=rp, out32=o32)

        def f_v(t4, hp=hp, V=V):
            wt, wv = wtile("v", vin, 0, 16, 2048 + hp * 512, 512)
            o32 = d["o_dav"][tok0 + t4 * 128:tok0 + (t4 + 1) * 128, hp * 512:(hp + 1) * 512] if P else None
            return v_unit([(h[:, kc, t4 * 128:(t4 + 1) * 128], wv[:, kc, :]) for kc in range(16)], [wt, (h, (t4 * 128) // 512)],
                          V[:, KC0 // 128 + t4, :], V, out32=o32)
        for hh in range(4):
            for ti, (t0, n) in enumerate(tl_q):
                units.append(("proj", lambda hh=hh, ti=ti, t0=t0, n=n: f_q(hh, ti, t0, n)))
        for hh in range(4):
            for ti, (t0, n) in enumerate(tl_kv):
                units.append(("proj", lambda hh=hh, ti=ti, t0=t0, n=n: f_k(hh, ti, t0, n)))
        for t4 in range(TKV0 // 128):
            units.append(("proj", lambda t4=t4: f_v(t4)))
        pipeline(units, {"proj": 4})
        units = []
        for hh in range(4):
            head = hp * 4 + hh
            if P:
                qtl = [(s * 256, 256, [2 * s, 2 * s + 1]) for s in range(2)]
            else:
                qtl = [(t0, n, list(range(NKT))) for (t0, n) in tl_q]
            for qi, (q0, n, keyts) in enumerate(qtl):
                d1 = dat[(hh * 2 + qi) % 4]
                for si in range(2):
                    lo = 64 * si

                    def sc_(kt, q0, n, hh=hh, lo=lo, kT=kT, qT=qT):
                        return ([(kT[lo:lo + 64, hh, kt * 128:(kt + 1) * 128], qT[lo:lo + 64, hh, q0:q0 + n])], [kT, qT], 64.0 ** -0.5)

                    def vf_(kt, hh=hh, V=V):
                        return V[:, kt, hh * 128:(hh + 1) * 128], [V]
                    if si == 0:
                        def fa_(q0, n, Ob, Db, d1=d1):
                            r1 = f32p.next()
                            recip(r1, r1[:, 0:n], Db, Db[:, 0:n], n)
                            tt(DVE, d1[:, 0:n], Ob[:, 0:n], r1[:, 0:n], ALU.mult, [Ob, r1], [d1])
                        units.append(("attn", lambda q0=q0, n=n, keyts=keyts, sc_=sc_, vf_=vf_, fa_=fa_: attn_unit(q0, n, keyts, sc_, vf_, fa_)))
                    else:
                        def fa_(q0, n, Ob, Db, d1=d1):
                            r2 = f32p.next()
                            recip(r2, r2[:, 0:n], Db, Db[:, 0:n], n)
                            tt(DVE, r2[:, 0:n], Ob[:, 0:n], r2[:, 0:n], ALU.mult, [Ob, r2], [r2])
                            stt(d1[:, 0:n], r2[:, 0:n], NEGLAM, d1[:, 0:n], ALU.mult, ALU.add, [r2, K_NEGLAM, d1], [d1])

                        def fb_(q0=q0, n=n, head=head, d1=d1):
                            sq = sqp.next()
                            act(sq[:, 0:n], d1[:, 0:n], AF.Square, [d1], [sq])
                            sb = SB4.next()
                            mm(sb, sb[:, 0:n], [(ONES, sq[:, 0:n])], [cm, sq])
                            r = rstd_from(sb, n, 128.0)
                            stt(oT[:, head, q0:q0 + n], d1[:, 0:n], SUBLN_EFF, r[:, 0:n], ALU.mult, ALU.mult, [d1, K_SUBLN, r], [oT])
                        units.append(("attn", lambda q0=q0, n=n, keyts=keyts, sc_=sc_, vf_=vf_, fa_=fa_, fb_=fb_:
                                      attn_unit(q0, n, keyts, sc_, vf_, fa_, fb_)))
            if first:
                units.append(("ada", lambda: ada_unit(2)))
        pipeline(units, {"attn": 2, "ada": 1})
        AR.release(qT)
        AR.release(kT)
        AR.release(V)
    for t in dat:
        AR.release(t)

    NK = KC0 + TKV0
    mqn = AR.alloc("mqn", [128, 4, TQ0], BF16)
    ckvT = AR.alloc("ckvT", [128, 4, NK], BF16)
    krgr = AR.alloc("krgr", [64, NK], F32)
    sqkr = AR.alloc("sqkr", [64, NK], BF16)
    if not P:
        dma(POOL, ckvT[:, :, 0:512], d["c_ckvT"].rearrange("(c p) t -> p c t", p=128), [], [ckvT])
        krc = AR.alloc("krc", [64, 512], F32)
        dma(SP, krc[:], d["c_krT"], [], [krc])
        act(sqkr[:, 0:512], krc[:], AF.Square, [krc], [sqkr])
        ts(krgr[:, 0:512], krc[:], spt[0:64, KNR:KNR + 1], None, ALU.mult, ALU.bypass, [krc, spt], [krgr])
        AR.release(krc)

    def norm4(col0, T, gcol, out_t, out_fn, out32_dram=None):
        wt, wv = wnext(vin, 0, 16, col0, 512)
        for ti, (t0, n) in enumerate(ntiles(T)):
            banks = [PR.next() for _ in range(4)]
            sb = PX.next()
            for c in range(4):
                mm(banks[c], banks[c][:, 0:n], [(wv[:, kc, c * 128:(c + 1) * 128], h[:, kc, t0:t0 + n]) for kc in range(16)], [wt, (h, ti)])
            for c in range(4):
                sq = sqp.next()
                act(sq[:, 0:n], banks[c][:, 0:n], AF.Square, [banks[c]], [sq])
                mm(sb, sb[:, 0:n], [(ONES, sq[:, 0:n])], [cm, sq], start=(c == 0), stop=(c == 3))
            r = rstd_from(sb, n, 512.0)
            for c in range(4):
                if out32_dram is None:
                    stt(out_fn(c, t0, n), banks[c][:, 0:n], spt[:, gcol + c:gcol + c + 1], r[:, 0:n], ALU.mult, ALU.mult,
                        [banks[c], spt, r], [out_t])
                else:
                    o32t = o32p.next()
                    stt(o32t[:, 0:n], banks[c][:, 0:n], spt[:, gcol + c:gcol + c + 1], r[:, 0:n], ALU.mult, ALU.mult,
                        [banks[c], spt, r], [o32t])
                    cpy(ACT, out_fn(c, t0, n), o32t[:, 0:n], [o32t], [out_t])
                    dma(SP, out32_dram(c, t0, n), o32t[:, 0:n], [o32t], [])
    norm4(3072, TQ0, MQN, mqn, lambda c, t0, n: mqn[:, c, t0:t0 + n])
    norm4(3584, TKV0, MKVN, ckvT, lambda c, t0, n: ckvT[:, c, KC0 + t0:KC0 + t0 + n],
          out32_dram=(lambda c, t0, n: d["o_ckv"][c * 128:(c + 1) * 128, tok0 + t0:tok0 + t0 + n]) if P else None)
    wt, wv = wnext(vin, 0, 16, 4096, 64)

    def kr_unit(ti, t0, n):
        bank = PR.next()
        mm(bank, bank[0:64, 0:n], [(wv[:, kc, 0:64], h[:, kc, t0:t0 + n]) for kc in range(16)], [wt, (h, ti)])
        yield
        act(sqkr[:, KC0 + t0:KC0 + t0 + n], bank[0:64, 0:n], AF.Square, [bank], [sqkr])
        if P:
            o32t = o32p.next()
            cpy(DVE, o32t[0:64, 0:n], bank[0:64, 0:n], [bank], [o32t])
            dma(SP, d["o_kr"][:, tok0 + t0:tok0 + t0 + n], o32t[0:64, 0:n], [o32t], [])
            ts(krgr[:, t0:t0 + n], bank[0:64, 0:n], spt[0:64, KNR:KNR + 1], None, ALU.mult, ALU.bypass, [bank, spt], [krgr])
        else:
            y = sqp.next()
            ts(y[0:64, 0:n], bank[0:64, 0:n], spt[0:64, KNR:KNR + 1], None, ALU.mult, ALU.bypass, [bank, spt], [y])
            yield from rope_tail(y, 64, n, cm[0:64, 3, 0:64], cosd[0:64, t0:t0 + n], sind[0:64, t0:t0 + n],
                                 krgr[:, KC0 + t0:KC0 + t0 + n], krgr)
    pipeline([("proj", lambda ti=ti, t0=t0, n=n: kr_unit(ti, t0, n)) for ti, (t0, n) in enumerate(tl_kv)], {"proj": 4})
    AR.release(h)

    tl_k = ntiles(NK)
    hb = [dict(kn=AR.alloc(f"kn{i}", [128, NK], BF16), krh=AR.alloc(f"krh{i}", [64, NK], BF16),
               vh=AR.alloc(f"vh{i}", [128, NK], BF16), qn=AR.alloc(f"qn{i}", [128, TQ0], BF16),
               qr=AR.alloc(f"qr{i}", [64, TQ0], BF16)) for i in range(2)]

    def mla_q_unit(wtq, wvq, t0, n, qn, qr):
        bank = PR.next()
        mm(bank, bank[:, 0:n], [(wvq[:, kc, 0:128], mqn[:, kc, t0:t0 + n]) for kc in range(4)], [wtq, mqn])
        bank2 = PR.next()
        mm(bank2, bank2[0:64, 0:n], [(wvq[:, kc, 128:192], mqn[:, kc, t0:t0 + n]) for kc in range(4)], [wtq, mqn])
        yield
        sq = sqp.next()
        act(sq[:, 0:n], bank[:, 0:n], AF.Square, [bank], [sq])
        sqr = sqp.next()
        act(sqr[0:64, 0:n], bank2[0:64, 0:n], AF.Square, [bank2], [sqr])
        sb = PX.next()
        mm(sb, sb[:, 0:n], [(ONES, sq[:, 0:n]), (cm[0:64, 0, :], sqr[0:64, 0:n])], [cm, sq, sqr])
        yield
        r = rstd_from(sb, n, 192.0)
        stt(qn[:, t0:t0 + n], bank[:, 0:n], spt[:, QNN:QNN + 1], r[:, 0:n], ALU.mult, ALU.mult, [bank, spt, r], [qn])
        if P:
            stt(qr[:, t0:t0 + n], bank2[0:64, 0:n], spt[0:64, QNR:QNR + 1], r[0:64, 0:n], ALU.mult, ALU.mult, [bank2, spt, r], [qr])
        else:
            y = sqp.next()
            stt(y[0:64, 0:n], bank2[0:64, 0:n], spt[0:64, QNR:QNR + 1], r[0:64, 0:n], ALU.mult, ALU.mult, [bank2, spt, r], [y])
            yield from rope_tail(y, 64, n, cm[0:64, 3, 0:64], cosd[0:64, t0:t0 + n], sind[0:64, t0:t0 + n], qr[:, t0:t0 + n], qr)

    def mla_v_unit(wtk, wvk, t4, vh):
        bank = PR.next()
        for i in range(4):
            mm(bank, bank[:, i * 128:(i + 1) * 128],
               [(ckvT[:, kc, (t4 + i) * 128:(t4 + i + 1) * 128], wvk[:, kc, 128:256]) for kc in range(4)], [wtk, ckvT])
        yield
        cpy(ACT, vh[:, t4 * 128:(t4 + 4) * 128], bank[:, :], [bank], [vh])

    def mla_prep(head):
        B_ = hb[head % 2]
        kn, krh, vh, qn, qr = B_["kn"], B_["krh"], B_["vh"], B_["qn"], B_["qr"]
        wh = {}

        def wk():
            if "k" not in wh:
                wh["k"] = wnext(vkvu, 0, 4, 256 * head, 256)
            return wh["k"]

        def wq():
            if "q" not in wh:
                wh["q"] = wnext(vqu, 0, 4, 192 * head, 192)
            return wh["q"]

        def f_k(t0, n):
            wtk, wvk = wk()

            def aft(r):
                tt(DVE, krh[:, t0:t0 + n], krgr[:, t0:t0 + n], r[0:64, 0:n], ALU.mult, [krgr, r], [krh])
            return qk_unit([(wvk[:, kc, 0:128], ckvT[:, kc, t0:t0 + n]) for kc in range(4)], [wtk, ckvT], n,
                           lambda sq: ([(ONES, sq[:, 0:n]), (cm[0:64, 0, :], sqkr[0:64, t0:t0 + n])], [sqkr]),
                           192.0, spt[:, KNN:KNN + 1], kn[:, t0:t0 + n], kn, after=aft)

        def f_v(t4):
            wtk, wvk = wk()
            return mla_v_unit(wtk, wvk, t4, vh)

        def f_q(t0, n):
            wtq, wvq = wq()
            return mla_q_unit(wtq, wvq, t0, n, qn, qr)
        units = []
        for (t0, n) in tl_k:
            units.append(("proj", lambda t0=t0, n=n: f_k(t0, n)))
        for t4 in range(0, NK // 128, 4):
            units.append(("proj", lambda t4=t4: f_v(t4)))
        for (t0, n) in tl_q:
            units.append(("q2", lambda t0=t0, n=n: f_q(t0, n)))
        return units

    def mla_attn(head):
        B_ = hb[head % 2]
        kn, krh, vh, qn, qr = B_["kn"], B_["krh"], B_["vh"], B_["qn"], B_["qr"]
        if P:
            qtl = [(s * 256, 256, [2 * s, 2 * s + 1]) for s in range(2)]
        else:
            qtl = [(t0, n, list(range(NK // 128))) for (t0, n) in tl_q]

        def sc_(kt, q0, n):
            return ([(kn[:, kt * 128:(kt + 1) * 128], qn[:, q0:q0 + n]), (krh[:, kt * 128:(kt + 1) * 128], qr[:, q0:q0 + n])],
                    [kn, krh, qn, qr], 192.0 ** -0.5)

        def vf_(kt):
            return vh[:, kt * 128:(kt + 1) * 128], [vh]

        def fa_(q0, n, Ob, Db):
            r1 = f32p.next()
            recip(r1, r1[:, 0:n], Db, Db[:, 0:n], n)
            tt(DVE, oT[:, 8 + head, q0:q0 + n], Ob[:, 0:n], r1[:, 0:n], ALU.mult, [Ob, r1], [oT])
        units = [("attn", lambda q0=q0, n=n, keyts=keyts: attn_unit(q0, n, keyts, sc_, vf_, fa_)) for (q0, n, keyts) in qtl]
        if first:
            units.append(("ada", lambda: ada_unit(3)))
        return units
    LIM = {"proj": 3, "q2": 2, "attn": 2, "ada": 1}
    pipeline(mla_prep(0), LIM)
    for head in range(8):
        if head + 1 < 8:
            pipeline(mla_prep(head + 1), LIM)
        pipeline(mla_attn(head), LIM)
    for B_ in hb:
        for t in B_.values():
            AR.release(t)
    for t in (mqn, ckvT, krgr, sqkr):
        AR.release(t)

    if first:
        ctx["ada_done"]()
    if not P:
        xres = AR.alloc("xres", [128, 16, TQM], F32, nsub=16, top=True)
        for c4 in range(4):
            dma(SP, xres[:, c4 * 4:c4 * 4 + 4, 0:640], xT[:, c4 * 4:c4 * 4 + 4, 0:640], [], [(xres, range(c4 * 4, c4 * 4 + 4))])
    ctx["dump"]("oT0", oT[:], [128, 16, TQ0], [oT], bf16=True)
    outproj(d["ab_w_out"][0], G1, DVK(0, cj, 2), oT, xres, TQ0)
    AR.release(oT)
    ctx["dump"]("x_attn0", xres[:], [128, 16, TQM], [xres])
    ffn(0, xres, TQ0)
    ctx["dump"]("x_l0", xres[:], [128, 16, TQM], [xres])

    A1, B1, G1 = DV(1, cj, 0), DV(1, cj, 1), DV(1, cj, 2)
    tl_kv = ntiles(TKV1)
    tl_q = ntiles(TQ1)
    h = AR.alloc("hL1", [128, 16, TKV1], BF16, nsub=len(tl_kv))
    rms_mod(lambda ti, t0, n: (xres, lambda c: xres[:, c, t0:t0 + n]), tl_kv, A1, B1, DVK(1, cj, 0), DVK(1, cj, 1), h)
    vin = wview(d["c_w_in"][0])
    NK = KC1 + TKV1
    NKT = NK // 128
    qT = AR.alloc("qTgq", [128, 16, TQ1], BF16)
    kT = AR.alloc("kTgq", [128, 4, NK], BF16)
    V = AR.alloc("Vgq", [128, NKT, 512], BF16)
    cosg, sing = (None, None) if P else (ropet[:, 0, :], ropet[:, 1, :])
    if not P:
        for i in range(2):
            dma(SP, ropet[:, i, :], d["rope"][:, 2 + i, :], [], [ropet])
        for kvh in range(4):
            dma(POOL, kT[:, kvh, 0:512], d["c_gqkT"][kvh], [], [kT])
        for t4 in range(4):
            dma(POOL, V[:, t4, :], d["c_gqv"][t4 * 128:(t4 + 1) * 128, :], [], [V])
    wh = {}

    def wtile1(key, c0):
        if key not in wh:
            wh[key] = wnext(vin, 0, 16, c0, 512)
        return wh[key]

    def f_k1(kvh, ti, t0, n):
        wt, wv = wtile1("k", 2048)
        rp = None if P else (RGQ, cosg[:, t0:t0 + n], sing[:, t0:t0 + n])
        o32 = d["o_gqk"][kvh, :, tok0 + t0:tok0 + t0 + n] if P else None
        return qk_unit([(wv[:, kc, kvh * 128:(kvh + 1) * 128], h[:, kc, t0:t0 + n]) for kc in range(16)], [wt, (h, ti)],
                       n, lambda sq: ([(ONES, sq[:, 0:n])], []), 128.0, spt[:, GQKN:GQKN + 1],
                       kT[:, kvh, KC1 + t0:KC1 + t0 + n], kT, rope=rp, out32=o32)

    def f_v1(t4):
        wt, wv = wtile1("v", 2560)
        o32 = d["o_gqv"][tok0 + t4 * 128:tok0 + (t4 + 1) * 128, :] if P else None
        return v_unit([(h[:, kc, t4 * 128:(t4 + 1) * 128], wv[:, kc, :]) for kc in range(16)], [wt, (h, (t4 * 128) // 512)],
                      V[:, KC1 // 128 + t4, :], V, out32=o32)

    def f_q1(jw, hh, ti, t0, n):
        wt, wv = wtile1(("q", jw), jw * 512)
        head = jw * 4 + hh
        rp = None if P else (RGQ, cosg[:, t0:t0 + n], sing[:, t0:t0 + n])
        return qk_unit([(wv[:, kc, hh * 128:(hh + 1) * 128], h[:, kc, t0:t0 + n]) for kc in range(16)], [wt, (h, ti)],
                       n, lambda sq: ([(ONES, sq[:, 0:n])], []), 128.0, spt[:, GQQN:GQQN + 1],
                       qT[:, head, t0:t0 + n], qT, rope=rp)
    units = []
    for kvh in range(4):
        for ti, (t0, n) in enumerate(tl_kv):
            units.append(("proj", lambda kvh=kvh, ti=ti, t0=t0, n=n: f_k1(kvh, ti, t0, n)))
    for t4 in range(TKV1 // 128):
        units.append(("proj", lambda t4=t4: f_v1(t4)))
    for jw in range(4):
        for hh in range(4):
            for ti, (t0, n) in enumerate(tl_q):
                units.append(("proj", lambda jw=jw, hh=hh, ti=ti, t0=t0, n=n: f_q1(jw, hh, ti, t0, n)))
    pipeline(units, {"proj": 4})
    AR.release(h)
    oT = AR.alloc("oT1", [128, 16, TQ1], BF16)
    if not P:
        maskb = AR.alloc("maskb", [128, 5, 512], BF16)
        dma(POOL, maskb[:], d["maskd"], [], [maskb])
    units = []
    for head in range(16):
        kvh = head // 4
        if P:
            qtl = [(s * 256, 256, [2 * s, 2 * s + 1]) for s in range(2)]
        else:
            qtl = [(0, 512, list(range(NKT)))]

        def sc_(kt, q0, n, head=head, kvh=kvh):
            pairs = [(kT[:, kvh, kt * 128:(kt + 1) * 128], qT[:, head, q0:q0 + n])]
            rds = [kT, qT]
            if (not P) and kt >= 4:
                pairs.append((IDENT, maskb[:, kt - 4, q0:q0 + n]))
                rds += [cm, maskb]
            return pairs, rds, 128.0 ** -0.5

        def vf_(kt, kvh=kvh):
            return V[:, kt, kvh * 128:(kvh + 1) * 128], [V]

        def fa_(q0, n, Ob, Db, head=head):
            r1 = f32p.next()
            ts(r1[:, 0:n], Db[:, 0:n], ES[:, head:head + 1], None, ALU.add, ALU.bypass, [Db, K_ES], [r1])
            r2 = f32p.next()
            recip(r2, r2[:, 0:n], r1, r1[:, 0:n], n)
            r1 = r2
            tt(DVE, oT[:, head, q0:q0 + n], Ob[:, 0:n], r1[:, 0:n], ALU.mult, [Ob, r1], [oT])
        for (q0, n, keyts) in qtl:
            units.append(("attn", lambda q0=q0, n=n, keyts=keyts, sc_=sc_, vf_=vf_, fa_=fa_: attn_unit(q0, n, keyts, sc_, vf_, fa_)))
    pipeline(units, {"attn": 2})
    AR.release(qT)
    AR.release(kT)
    AR.release(V)
    if not P:
        AR.release(maskb)
    outproj(d["c_w_out"][0], G1, DVK(1, cj, 2), oT, xres, TQ1)
    AR.release(oT)
    ffn(1, xres, TQ1)
    for c4 in range(4):
        dma(SP, yT[:, c4 * 4:c4 * 4 + 4, 0:TQ1], xres[:, c4 * 4:c4 * 4 + 4, 0:TQ1], [(xres, range(c4 * 4, c4 * 4 + 4))], [])
    AR.release(xres)
    if not P:
        AR.release(ropet)
    for p_ in (sqp, f32p, ep, lnp, rsp, o32p):
        p_.free()


def _consts():
    cm = np.zeros((128, 5, 128), np.float32)
    cm[:, 0, :] = 1.0
    k = np.arange(128)
    cm[:, 1, :] = (k[:, None] // 64 == k[None, :] // 64)
    cm[:, 2, :] = np.eye(128)
    for m in range(128):
        j = m % 32
        if j < 16:
            cm[m + 16, 3, m] = -1.0
        else:
            cm[m - 16, 3, m] = 1.0
        j = m % 64
        if j < 32:
            cm[m + 32, 4, m] = -1.0
        else:
            cm[m - 32, 4, m] = 1.0
    return cm


def _rope_tables(half):
    t = np.arange(1024)
    pos = t if half == 0 else 1023 - t
    row = (pos // 64).astype(np.float32)
    col = (pos % 64).astype(np.float32)
    out = np.zeros((128, 4, 1024), np.float32)
    for p in range(128):
        i = p % 64
        f = (i % 32) % 16
        inv = np.float32(10000.0) ** (-np.float32(f) / np.float32(16))
        ang = (row if (i // 32) == 0 else col) * np.float32(inv)
        out[p, 0] = np.cos(ang)
        out[p, 1] = np.sin(ang)
        f = (p % 64) % 32
        inv = np.float32(10000.0) ** (-np.float32(f) / np.float32(32))
        ang = (row if (p // 64) == 0 else col) * np.float32(inv)
        out[p, 2] = np.cos(ang)
        out[p, 3] = np.sin(ang)
    return out


def _mask():
    m = np.zeros((128, 5, 512), np.float32)
    q = np.arange(512)[None, :]
    for lt in range(5):
        k = (lt * 128 + np.arange(128))[:, None]
        m[:, lt, :] = np.where(np.abs(q - k) <= 128, 0.0, NEG)
    return m


def _pack_sp(inp, cond0, cond1):
    sp = np.zeros((128, NSP), np.float32)

    def fm(v):
        return np.asarray(v, np.float32).reshape(-1, 128).T
    for l in range(2):
        sp[:, N1G + l * 16:N1G + l * 16 + 16] = fm(inp["norm1_g"][l])
        sp[:, N2G + l * 16:N2G + l * 16 + 16] = fm(inp["norm2_g"][l])
        b = fm(inp["ada_b"][l])
        sp[:, ADAB + l * 192:ADAB + (l + 1) * 192] = np.repeat(b, 2, axis=1)
    sp[:, DAQN] = np.tile(inp["da_q_norm"][0], 2)
    sp[:, DAKN] = np.tile(inp["da_k_norm"][0], 2)
    sp[:, SUBLN] = inp["da_subln"][0]
    sp[:, MQN:MQN + 4] = fm(inp["mla_q_a_norm"][0])
    sp[:, MKVN:MKVN + 4] = fm(inp["mla_kv_a_norm"][0])
    sp[:, QNN] = inp["mla_q_norm"][0][:128]
    sp[:64, QNR] = inp["mla_q_norm"][0][128:]
    sp[:, KNN] = inp["mla_k_norm"][0][:128]
    sp[:64, KNR] = inp["mla_k_norm"][0][128:]
    sp[:, GQQN] = inp["gq_q_norm"][0]
    sp[:, GQKN] = inp["gq_k_norm"][0]
    sp[:, SINK:SINK + 16] = np.broadcast_to(inp["gq_sink"][0][None, :], (128, 16))
    sp[:64, LAMC + 0] = inp["da_lambda_q1"][0]
    sp[:64, LAMC + 1] = inp["da_lambda_k1"][0]
    sp[:64, LAMC + 2] = inp["da_lambda_q2"][0]
    sp[:64, LAMC + 3] = inp["da_lambda_k2"][0]
    c0, c1 = fm(cond0), fm(cond1)
    sp[:, CONDT:CONDT + 32:2] = c0
    sp[:, CONDT + 1:CONDT + 32:2] = c1
    return sp


_WKEYS = ["ada_w", "ff1_w", "ff2_w", "ab_w_in", "ab_w_out", "c_w_in", "c_w_out"]


def core_inputs(inp, c, shared):
    sb, half = c // 2, c % 2
    m = dict(shared)
    xp = np.asarray(inp["x_prompt"][4 * c:4 * c + 4]).reshape(1024, 2048)
    m["xTp"] = np.ascontiguousarray(xp.T)
    xs = np.asarray(inp["x_sample"][sb])
    if half == 1:
        xs = xs[::-1]
    m["xTs"] = np.ascontiguousarray(xs.T)
    m["c_dakT"] = np.ascontiguousarray(np.transpose(inp["cache_da_k"][sb, 0], (1, 2, 0)))
    m["c_dav"] = np.ascontiguousarray(np.asarray(inp["cache_da_v"][sb, 0]).reshape(512, 1024))
    m["c_ckvT"] = np.ascontiguousarray(np.asarray(inp["cache_mla_ckv"][sb, 0]).T)
    m["c_krT"] = np.ascontiguousarray(np.asarray(inp["cache_mla_krope"][sb, 0]).T)
    m["c_gqkT"] = np.ascontiguousarray(np.transpose(inp["cache_gq_k"][sb, 0], (1, 2, 0)))
    m["c_gqv"] = np.ascontiguousarray(np.asarray(inp["cache_gq_v"][sb, 0]).reshape(512, 512))
    m["sp"] = _pack_sp(inp, inp["c_ctx"], inp["c"][sb])
    m["rope"] = shared["_rope"][half]
    del m["_rope"]
    return m


def shared_inputs(inp):
    sh = {k: np.asarray(inp[k], np.float32) for k in _WKEYS}
    sh["w_q_up"] = np.asarray(inp["mla_w_q_up"], np.float32)
    sh["w_kv_up"] = np.asarray(inp["mla_w_kv_up"], np.float32)
    sh["cmat"] = _consts()
    sh["maskl1"] = _mask()
    sh["_rope"] = [_rope_tables(0), _rope_tables(1)]
    return sh


_PROG = {}


def kernel(**inp):
    if "nc" not in _PROG:
        _PROG["nc"] = build_program()[0]
    nc = _PROG["nc"]
    inp = {k: np.asarray(v) for k, v in inp.items()}
    sh = shared_inputs(inp)
    in_maps = [core_inputs(inp, c, sh) for c in range(8)]
    res = run_bass_kernel_spmd(nc, in_maps, core_ids=list(range(8)))
    R = res.results
    y_prompt = np.zeros((32, 256, 2048), np.float32)
    y_sample = np.zeros((4, 1024, 2048), np.float32)
    n_dak = np.zeros((32, 1, 256, 8, 128), np.float32)
    n_dav = np.zeros((32, 1, 256, 8, 128), np.float32)
    n_ckv = np.zeros((32, 1, 256, 512), np.float32)
    n_kr = np.zeros((32, 1, 256, 64), np.float32)
    n_gqk = np.zeros((32, 1, 256, 4, 128), np.float32)
    n_gqv = np.zeros((32, 1, 256, 4, 128), np.float32)
    for c in range(8):
        r = R[c]
        sb, half = c // 2, c % 2
        y_prompt[4 * c:4 * c + 4] = r["yTp"].T.reshape(4, 256, 2048)
        ys = r["yTs"].T
        if half == 0:
            y_sample[sb, 0:512] = ys
        else:
            y_sample[sb, 512:1024] = ys[::-1]
        n_dak[4 * c:4 * c + 4, 0] = np.transpose(r["o_dak"], (2, 0, 1)).reshape(4, 256, 8, 128)
        n_dav[4 * c:4 * c + 4, 0] = r["o_dav"].reshape(4, 256, 8, 128)
        n_ckv[4 * c:4 * c + 4, 0] = r["o_ckv"].T.reshape(4, 256, 512)
        n_kr[4 * c:4 * c + 4, 0] = r["o_kr"].T.reshape(4, 256, 64)
        n_gqk[4 * c:4 * c + 4, 0] = np.transpose(r["o_gqk"], (2, 0, 1)).reshape(4, 256, 4, 128)
        n_gqv[4 * c:4 * c + 4, 0] = r["o_gqv"].reshape(4, 256, 4, 128)
    return (y_prompt, y_sample, n_dak, n_dav, n_ckv, n_kr, n_gqk, n_gqv)
```
